# Optimizing a Trainium2 kernel written in Bass

```python
import math
import jax, jax.numpy as jnp
from jax import lax
import numpy as np

D_MODEL = 1024
BATCH = 8
SEQ = 4096
DEPTH = 1

CHUNK = 64
Q_BLOCK = 128
D_MIX = D_MODEL
MLA_HEADS = 8
NOPE_DIM = 64
ROPE_DIM = 32
V_DIM = 64
Q_LORA = 256
KV_LORA = 128
ROPE_THETA = 10000.0
MLA_WIDTH = MLA_HEADS * V_DIM
ATTN_SCALE = (NOPE_DIM + ROPE_DIM) ** -0.5
GM_HEADS = 8
GM_DIM = 64
GM_CHUNK = 128
GM_WIDTH = GM_HEADS * GM_DIM
IN_COLS = Q_LORA + KV_LORA + ROPE_DIM + 2 * GM_WIDTH
D_FF = 2816
CONV_W = 3
EPS = 1e-6

kernel_name = "hybrid_mla_gmlp_convffn_block"


def rms_norm(x, g):
    x32 = x.astype(jnp.float32)
    y = x32 * lax.rsqrt(jnp.mean(x32 * x32, axis=-1, keepdims=True) + EPS)
    return (y * g.astype(jnp.float32)).astype(x.dtype)


def layer_norm(x, g, b):
    x32 = x.astype(jnp.float32)
    mu = jnp.mean(x32, axis=-1, keepdims=True)
    var = jnp.mean(jnp.square(x32 - mu), axis=-1, keepdims=True)
    y = (x32 - mu) * lax.rsqrt(var + EPS)
    return (y * g.astype(jnp.float32) + b.astype(jnp.float32)).astype(x.dtype)


def modulate(h, shift, scale):
    return h * (1.0 + scale[:, None, :]) + shift[:, None, :]


def rope_tables(seq):
    pos = jnp.arange(seq, dtype=jnp.float32)
    inv = ROPE_THETA ** (-jnp.arange(0, ROPE_DIM, 2, dtype=jnp.float32) / ROPE_DIM)
    ang = pos[:, None] * inv[None, :]
    return jnp.cos(ang), jnp.sin(ang)


def apply_rope(t, cos, sin):
    t32 = t.astype(jnp.float32)
    t1, t2 = jnp.split(t32, 2, axis=-1)
    out = jnp.concatenate([t1 * cos - t2 * sin, t2 * cos + t1 * sin], axis=-1)
    return out.astype(t.dtype)


def mla_attention(q_nope, q_rope, k_nope, k_rope, v):
    B, S, H, _ = q_nope.shape
    nb = S // Q_BLOCK
    k_chunk = jnp.arange(S) // CHUNK

    def to_blocks(t):
        return t.reshape((B, nb, Q_BLOCK) + t.shape[2:]).swapaxes(0, 1)

    def one_block(args):
        qn, qr, bi = args
        s = (jnp.einsum('bqhd,bkhd->bhqk', qn, k_nope)
             + jnp.einsum('bqhr,bkr->bhqk', qr, k_rope))
        s = s.astype(jnp.float32) * ATTN_SCALE
        q_chunk = (bi * Q_BLOCK + jnp.arange(Q_BLOCK)) // CHUNK
        mask = k_chunk[None, :] <= q_chunk[:, None]
        s = jnp.where(mask[None, None], s, -jnp.inf)
        p = jax.nn.softmax(s, axis=-1).astype(v.dtype)
        return jnp.einsum('bhqk,bkhd->bqhd', p, v)

    o = lax.map(one_block, (to_blocks(q_nope), to_blocks(q_rope), jnp.arange(nb)))
    return o.swapaxes(0, 1).reshape(B, S, H * V_DIM)


def gmlp_spatial_gate(u, v, ln_g, ln_b, w_s, b_s):
    B, S, H, Dh = v.shape
    v = layer_norm(v, ln_g, ln_b)
    idx = jnp.arange(GM_CHUNK) // CHUNK
    mask = (idx[None, :] <= idx[:, None]).astype(w_s.dtype)
    w_m = w_s * mask[None]
    vb = v.reshape(B, S // GM_CHUNK, GM_CHUNK, H, Dh)
    mixed = jnp.einsum('hij,bnjhd->bnihd', w_m, vb) + b_s.T[None, None, :, :, None]
    return u * mixed.reshape(B, S, H, Dh)


def token_mixer(h, w_in, g_q, w_uq, g_kv, w_ukv, gm_ln_g, gm_ln_b, w_spatial, b_spatial, w_out):
    B, S, _ = h.shape
    z = h @ w_in
    o1 = Q_LORA
    o2 = o1 + KV_LORA
    o3 = o2 + ROPE_DIM
    c_q, c_kv, k_r, g_uv = z[..., :o1], z[..., o1:o2], z[..., o2:o3], z[..., o3:]
    q = (rms_norm(c_q, g_q) @ w_uq).reshape(B, S, MLA_HEADS, NOPE_DIM + ROPE_DIM)
    kv = (rms_norm(c_kv, g_kv) @ w_ukv).reshape(B, S, MLA_HEADS, NOPE_DIM + V_DIM)
    q_nope, q_rope = q[..., :NOPE_DIM], q[..., NOPE_DIM:]
    k_nope, v = kv[..., :NOPE_DIM], kv[..., NOPE_DIM:]
    cos, sin = rope_tables(S)
    q_rope = apply_rope(q_rope, cos[None, :, None, :], sin[None, :, None, :])
    k_rope = apply_rope(k_r, cos[None], sin[None])
    attn = mla_attention(q_nope, q_rope, k_nope, k_rope, v)
    g = jax.nn.gelu(g_uv)
    u = g[..., :GM_WIDTH].reshape(B, S, GM_HEADS, GM_DIM)
    vv = g[..., GM_WIDTH:].reshape(B, S, GM_HEADS, GM_DIM)
    sgu = gmlp_spatial_gate(u, vv, gm_ln_g, gm_ln_b, w_spatial, b_spatial).reshape(B, S, GM_WIDTH)
    return jnp.concatenate([attn, sgu], axis=-1) @ w_out


def conv_ffn(h, w_up, conv_w, conv_b, w_down):
    S = h.shape[1]
    up = h @ w_up
    upp = jnp.pad(up, ((0, 0), (CONV_W - 1, 0), (0, 0)))
    y = conv_b + sum(upp[:, k:k + S, :] * conv_w[k] for k in range(CONV_W))
    a, b = y[..., :D_FF], y[..., D_FF:]
    return (jax.nn.silu(a) * b) @ w_down


def setup_inputs(seed: int = 0) -> dict:
    key = jax.random.key(seed)
    ks = jax.random.split(key, 24)
    f32 = jnp.float32
    n = lambda k, shape, s: jax.random.normal(k, shape, f32) * s
    gain = lambda k, shape: 1.0 + 0.05 * jax.random.normal(k, shape, f32)
    L = DEPTH
    return {
        "x": jax.random.normal(ks[0], (BATCH, SEQ, D_MODEL), f32),
        "c": jax.random.normal(ks[1], (BATCH, D_MODEL), f32),
        "w_ada": n(ks[2], (L, D_MODEL, 6 * D_MODEL), 0.5 * D_MODEL ** -0.5),
        "b_ada": n(ks[3], (L, 6 * D_MODEL), 0.02),
        "g_pre_mix": gain(ks[4], (L, D_MODEL)),
        "g_post_mix": gain(ks[5], (L, D_MODEL)),
        "w_in": n(ks[6], (L, D_MODEL, IN_COLS), D_MODEL ** -0.5),
        "g_q": gain(ks[7], (L, Q_LORA)),
        "w_uq": n(ks[8], (L, Q_LORA, MLA_HEADS * (NOPE_DIM + ROPE_DIM)), Q_LORA ** -0.5),
        "g_kv": gain(ks[9], (L, KV_LORA)),
        "w_ukv": n(ks[10], (L, KV_LORA, MLA_HEADS * (NOPE_DIM + V_DIM)), KV_LORA ** -0.5),
        "gm_ln_g": gain(ks[11], (L, GM_HEADS, GM_DIM)),
        "gm_ln_b": n(ks[12], (L, GM_HEADS, GM_DIM), 0.02),
        "w_spatial": n(ks[13], (L, GM_HEADS, GM_CHUNK, GM_CHUNK), GM_CHUNK ** -0.5),
        "b_spatial": 1.0 + n(ks[14], (L, GM_HEADS, GM_CHUNK), 0.05),
        "w_out": n(ks[15], (L, D_MIX, D_MODEL), D_MIX ** -0.5),
        "g_pre_ffn": gain(ks[16], (L, D_MODEL)),
        "g_post_ffn": gain(ks[17], (L, D_MODEL)),
        "w_up": n(ks[18], (L, D_MODEL, 2 * D_FF), D_MODEL ** -0.5),
        "conv_w": n(ks[19], (L, CONV_W, 2 * D_FF), CONV_W ** -0.5),
        "conv_b": n(ks[20], (L, 2 * D_FF), 0.02),
        "w_down": n(ks[21], (L, D_FF, D_MODEL), D_FF ** -0.5),
    }


def reference(x, c, w_ada, b_ada, g_pre_mix, g_post_mix, w_in, g_q, w_uq, g_kv, w_ukv,
              gm_ln_g, gm_ln_b, w_spatial, b_spatial, w_out, g_pre_ffn, g_post_ffn,
              w_up, conv_w, conv_b, w_down):
    c_act = jax.nn.silu(c)
    for l in range(DEPTH):
        ada = c_act @ w_ada[l] + b_ada[l]
        sh1, sc1, gt1, sh2, sc2, gt2 = jnp.split(ada, 6, axis=-1)
        h = modulate(rms_norm(x, g_pre_mix[l]), sh1, sc1)
        m = token_mixer(h, w_in[l], g_q[l], w_uq[l], g_kv[l], w_ukv[l], gm_ln_g[l], gm_ln_b[l],
                        w_spatial[l], b_spatial[l], w_out[l])
        x = x + gt1[:, None, :] * rms_norm(m, g_post_mix[l])
        h = modulate(rms_norm(x, g_pre_ffn[l]), sh2, sc2)
        f = conv_ffn(h, w_up[l], conv_w[l], conv_b[l], w_down[l])
        x = x + gt2[:, None, :] * rms_norm(f, g_post_ffn[l])
    return x
```

```python
import bisect
import numpy as np
import concourse.bass as bass
import concourse.mybir as mybir
from concourse.bass_utils import run_bass_kernel_spmd

F32 = mybir.dt.float32
F32R = mybir.dt.float32r
BF16 = mybir.dt.bfloat16
ALU = mybir.AluOpType
AF = mybir.ActivationFunctionType
AX = mybir.AxisListType

ENGS = ("pe", "act", "dve", "pool", "sp")
DTSZ = {F32: 4, F32R: 4, BF16: 2}

S = 4096
D = 1024
NHEAD = 8
DFF = 2816
NCH = 22
ATTN_SCALE = 96.0 ** -0.5
EPS = 1e-6
N_CORES = 8


class View:
    __slots__ = ("ap", "reg", "extra")

    def __init__(self, ap, reg):
        self.ap = ap
        self.reg = reg
        self.extra = ()

    def re(self, pat, **kw):
        return View(self.ap.rearrange(pat, **kw), self.reg)

    def also(self, *regs):
        v = View(self.ap, self.reg)
        v.extra = list(regs)
        return v


class Root:
    def __init__(self, prog, h, P, F, kind):
        self.prog = prog
        self.h = h
        self.P = P
        self.F = F
        self.kind = kind
        self.id = len(prog.roots)
        prog.roots.append(self)
        self.recs = []

    def v(self, p0=0, p1=None, f0=0, f1=None):
        p1 = self.P if p1 is None else p1
        f1 = self.F if f1 is None else f1
        assert 0 <= p0 < p1 <= self.P and 0 <= f0 < f1 <= self.F, (p0, p1, f0, f1)
        if self.kind == "ps":
            reg = (self.id, 0, self.P, (f0 // 512) * 512, ((f1 + 511) // 512) * 512)
        else:
            reg = (self.id, p0, p1, f0, f1)
        return View(self.h[p0:p1, f0:f1], reg)


class Sub:
    def __init__(self, parent, off, F, dt):
        assert off % 4 == 0
        self.parent = parent
        self.off = off
        self.F = F
        self.dt = dt
        self.sz = DTSZ[dt]
        assert off + F * self.sz <= parent.F * 2, (off, F, self.sz, parent.F * 2)

    def v(self, p0=0, p1=128, f0=0, f1=None):
        f1 = self.F if f1 is None else f1
        assert 0 <= p0 < p1 <= 128 and 0 <= f0 < f1 <= self.F, (p0, p1, f0, f1, self.F)
        b0 = self.off + f0 * self.sz
        b1 = self.off + f1 * self.sz
        assert b0 % 2 == 0 and b1 % 2 == 0
        ap = self.parent.h[p0:p1, b0 // 2:b1 // 2]
        if self.dt != BF16:
            ap = ap.bitcast(self.dt)
        return View(ap, (self.parent.id, p0, p1, b0 // 2, b1 // 2))

    def alias(self, dt):
        return Sub(self.parent, self.off, self.F * self.sz // DTSZ[dt], dt)


class PB:
    def __init__(self, root, bank, dt=None):
        self.root = root
        self.bank = bank
        self.dt = dt or F32
        self.F = 512 if self.dt == F32 else 1024

    def v(self, p0=0, p1=128, f0=0, f1=None):
        f1 = self.F if f1 is None else f1
        base = self.root.h[p0:p1, self.bank * 512:(self.bank + 1) * 512]
        if self.dt == BF16:
            base = base.bitcast(BF16)
        return View(base[:, f0:f1], (self.root.id, 0, 128, self.bank * 512, (self.bank + 1) * 512))

    def as_bf16(self):
        return PB(self.root, self.bank, BF16)


class Op:
    __slots__ = ("eng", "fn", "reads", "writes", "tag", "idx", "deps", "inc", "semval", "waits")


class Prog:
    def __init__(self, nc):
        self.nc = nc
        self.roots = []
        self.ops = []

    def sbuf_root(self, name, ncols):
        h = self.nc.alloc_sbuf_tensor(name, [128, ncols], BF16)
        return Root(self, h, 128, ncols, "sb")

    def ps(self, name, F, dt=F32):
        h = self.nc.alloc_psum_tensor(name, [128, F], dt)
        return Root(self, h, 128, F, "ps")

    def dram(self, name, shape, dt, kind):
        h = self.nc.dram_tensor(name, list(shape), dt, kind=kind)
        R = shape[0]
        C = int(np.prod(shape[1:])) if len(shape) > 1 else 1
        return Root(self, h, R, C, "dram")

    def op(self, eng, fn, reads=(), writes=(), tag=None):
        o = Op()
        o.eng = eng
        o.fn = fn
        o.reads = [r.reg for r in reads] + [x for r in reads for x in r.extra]
        o.writes = [w.reg for w in writes] + [x for w in writes for x in w.extra]
        o.tag = tag
        o.idx = len(self.ops)
        o.deps = set()
        o.inc = False
        o.semval = 0
        o.waits = []
        self.ops.append(o)
        self._track(o)
        return o

    def dma(self, out, in_, tag, **kw):
        return self.op("sp", lambda e: e.dma_start(out=out.ap, in_=in_.ap, **kw), [in_], [out], tag=tag)

    def _track(self, o):
        ops = self.ops
        roots = self.roots
        for reg in o.reads:
            r0, r1, r2, r3 = reg[1:]
            for rec in roots[reg[0]].recs:
                if rec[5] and rec[0] < r1 and r0 < rec[1] and rec[2] < r3 and r2 < rec[3]:
                    o.deps.add(rec[4])
        for reg in o.writes:
            root = roots[reg[0]]
            r0, r1, r2, r3 = reg[1:]
            keep = []
            for rec in root.recs:
                if rec[0] < r1 and r0 < rec[1] and rec[2] < r3 and r2 < rec[3]:
                    if rec[4] != o.idx:
                        o.deps.add(rec[4])
                    if r0 <= rec[0] and rec[1] <= r1 and r2 <= rec[2] and rec[3] <= r3:
                        continue
                keep.append(rec)
            root.recs = keep
        for reg in o.reads:
            root = roots[reg[0]]
            r = reg[1:]
            if o.eng != "sp":
                root.recs = [
                    rec for rec in root.recs
                    if rec[5] or rec[:4] != r or ops[rec[4]].eng != o.eng
                ]
            root.recs.append((r[0], r[1], r[2], r[3], o.idx, False))
        for reg in o.writes:
            r = reg[1:]
            roots[reg[0]].recs.append((r[0], r[1], r[2], r[3], o.idx, True))

    def finalize(self):
        ops = self.ops
        per_eng = {e: [] for e in ENGS}
        for o in ops:
            per_eng[o.eng].append(o)
        needed = []
        for o in ops:
            best = {}
            for d in o.deps:
                p = ops[d]
                if p.eng == "pe" and o.eng == "pe":
                    continue
                key = ("tag", p.tag) if p.eng == "sp" else ("eng", p.eng)
                if key not in best or best[key] < p.idx:
                    best[key] = p.idx
            needed.append(best)
            for pidx in best.values():
                ops[pidx].inc = True
        for e in ("pe", "act", "dve", "pool"):
            c = 0
            for o in per_eng[e]:
                if o.inc:
                    c += 1
                    o.semval = c
        tag_lists = {}
        for o in per_eng["sp"]:
            tag_lists.setdefault(o.tag, []).append(o.idx)
        self.tags = sorted(tag_lists.keys())
        waited = {e: {} for e in ENGS}
        for o in ops:
            w = []
            for key, pidx in needed[o.idx].items():
                if key[0] == "eng":
                    val = ops[pidx].semval
                else:
                    val = 16 * bisect.bisect_left(tag_lists[key[1]], o.idx)
                if waited[o.eng].get(key, 0) >= val:
                    continue
                waited[o.eng][key] = val
                w.append((key, val))
            o.waits = w
        self.per_eng = per_eng
        self.tag_totals = {t: len(l) for t, l in tag_lists.items()}

    def emit(self):
        nc = self.nc
        sems = {}
        for e in ("pe", "act", "dve", "pool"):
            sems[("eng", e)] = nc.alloc_semaphore(name=f"s_{e}")
        for t in self.tags:
            sems[("tag", t)] = nc.alloc_semaphore(name=f"d_{t}")
        per_eng = self.per_eng
        totals = self.tag_totals

        def run(eng_name, e):
            for o in per_eng[eng_name]:
                for key, val in o.waits:
                    e.wait_ge(sems[key], val)
                ins = o.fn(e)
                if eng_name == "sp":
                    ins.then_inc(sems[("tag", o.tag)], 16)
                elif o.inc:
                    ins.then_inc(sems[("eng", eng_name)], 1)
            if eng_name == "sp":
                for t, c in totals.items():
                    e.wait_ge(sems[("tag", t)], 16 * c)

        with nc.Block() as block:
            @block.sync
            def _(e):
                run("sp", e)

            @block.tensor
            def _(e):
                run("pe", e)

            @block.scalar
            def _(e):
                run("act", e)

            @block.vector
            def _(e):
                run("dve", e)

            @block.gpsimd
            def _(e):
                run("pool", e)


def build(NT=8, phase2=True):
    nc = bass.Bass("TRN2", target_bir_lowering=False)
    P = Prog(nc)
    SEQ = NT * 512

    def din(name, shape):
        return P.dram(name, shape, F32, "ExternalInput")

    x_d = din("x", [S, D])
    ccol_d = din("ccol", [128, 8])
    wada_d = din("w_ada", [D, 6 * D])
    bada_d = din("b_ada", [1, 6 * D])
    gpre1_d = din("gpre1", [128, 8])
    gpre2_d = din("gpre2", [128, 8])
    gpost1_d = din("gpost1", [1, D])
    gpost2_d = din("gpost2", [1, D])
    win_d = din("w_in", [D, 1440])
    gq_d = din("gq", [128, 2])
    wuq_d = din("w_uq", [256, 768])
    gkv_d = din("gkv", [128, 1])
    wukv_d = din("w_ukv", [128, 1024])
    lng_d = din("lng", [1, 512])
    lnb_d = din("lnb", [1, 512])
    wsT_d = din("wsT", [128, 1024])
    bs_d = din("bs", [128, 8])
    wout_d = din("w_out", [D, D])
    wup_d = din("w_up", [D, 2 * DFF])
    convw_d = din("convw", [128, 132])
    convb_d = din("convb", [128, 44])
    wdn_d = din("w_down", [DFF, D])
    cos_d = din("cosT", [128, S])
    sin_d = din("sinT", [128, S])
    out_d = P.dram("out", [S, D], F32, "ExternalOutput")

    SBCOLS = 106000
    SB = P.sbuf_root("SB", SBCOLS)
    off = [0]

    def carve(F, dt, at=None):
        if at is None:
            at = off[0]
            off[0] = at + ((F * DTSZ[dt] + 3) // 4) * 4
        return Sub(SB, at, F, dt)

    KT = carve(8 * 4096, BF16)
    VA = carve(32 * 520 + 64, BF16)
    WIN = carve(8 * 1472, BF16)
    WUQ = carve(2 * 1024, BF16)
    WUKV = carve(1024, BF16)
    WOUT = carve(8 * 1024, BF16)
    WST = carve(8 * 128, BF16)
    PH2_END = 45056 + 90112
    assert off[0] >= PH2_END
    IDENT = carve(128, BF16)
    NEGH = carve(512, F32)
    GG1 = carve(1024, F32)
    GG2 = carve(1024, F32)
    COLS = carve(64, F32)
    GBC = carve(512, F32)
    CST = carve(512, F32)
    BSC = carve(8, F32)
    CONVW = carve(132, F32)
    CONVB = carve(44, F32)
    ST = carve(128, F32)
    ONESF = carve(128, F32)
    IDENTF = carve(128, F32)
    CARRY = [carve(44 * 2, F32) for _ in range(2)]
    R0 = off[0]
    RSZ = SBCOLS * 2 - R0

    def rcarve(base, F, dt):
        b = R0 + base
        assert base + F * DTSZ[dt] <= RSZ, (base, F, RSZ)
        return Sub(SB, b, F, dt)

    WDN = Sub(SB, 0, 22 * 1024, BF16)
    WUP = Sub(SB, 45056, 8 * 5632, BF16)

    RT_, RM_, RS_, RO_ = (P.ps(n, 1024) for n in ("psT", "psM", "psS", "psO"))
    PT = [PB(RT_, i, BF16) for i in range(2)]
    PM = [PB(RM_, i) for i in range(2)]
    PSS = [PB(RS_, i) for i in range(2)]
    PO = [PB(RO_, i) for i in range(2)]

    def rd(*vs):
        return [v for v in vs if isinstance(v, View)]

    def A(x):
        return x.ap if isinstance(x, View) else x

    def mm(out, lhsT, rhs, start, stop):
        P.op("pe", lambda e: e.matmul(out.ap, lhsT=lhsT.ap, rhs=rhs.ap, start=start, stop=stop), [lhsT, rhs], [out])

    def tr(out, in_):
        idv = IDENT.v(0, in_.ap.shape[0], 0, in_.ap.shape[0])
        P.op("pe", lambda e: e.transpose(out=out.ap, in_=in_.ap, identity=idv.ap), [in_, idv], [out])

    def act(out, in_, func, scale=1.0, bias=None, accum=None):
        kw = {}
        if bias is not None:
            kw["bias"] = A(bias)
        if accum is not None:
            kw["accum_out"] = accum.ap
        P.op("act", lambda e: e.activation(out=out.ap, in_=in_.ap, func=func, scale=A(scale), **kw),
             [in_] + rd(scale, bias), [out] + rd(accum))

    def ts(eng, out, in0, s1, s2, op0, op1=None):
        kw = {} if op1 is None else {"op1": op1}
        P.op(eng, lambda e: e.tensor_scalar(out=out.ap, in0=in0.ap, scalar1=A(s1), scalar2=A(s2), op0=op0, **kw),
             [in0] + rd(s1, s2), [out])

    def tt(eng, out, in0, in1, op):
        P.op(eng, lambda e: e.tensor_tensor(out=out.ap, in0=in0.ap, in1=in1.ap, op=op), [in0, in1], [out])

    def stt(out, in0, scalar, in1, op0, op1):
        P.op("dve", lambda e: e.scalar_tensor_tensor(out=out.ap, in0=in0.ap, scalar=A(scalar), in1=in1.ap, op0=op0, op1=op1),
             [in0, in1] + rd(scalar), [out])

    def cp(eng, out, in_):
        if eng == "act":
            P.op("act", lambda e: e.activation(out=out.ap, in_=in_.ap, func=AF.Copy), [in_], [out])
        else:
            P.op(eng, lambda e: e.tensor_copy(out=out.ap, in_=in_.ap), [in_], [out])

    def memset(eng, out, val):
        P.op(eng, lambda e: e.memset(out.ap, val), [], [out])

    def red(out, in_, op=ALU.add):
        P.op("dve", lambda e: e.tensor_reduce(out=out.ap, in_=in_.ap, axis=AX.X, op=op), [in_], [out])

    def recip(out, in_):
        P.op("dve", lambda e: e.reciprocal(out=out.ap, in_=in_.ap), [in_], [out])

    def bview(v, pat_shape):
        return View(v.ap.to_broadcast(pat_shape), v.reg)

    def rstd_from(out, ss, n, width):
        ts("dve", out, ss, 1.0 / n, EPS, ALU.mult, ALU.add)
        p0, p1 = out.reg[1], out.reg[2]
        tt("pool", out, out, NEGH.v(p0, p1, 0, width), ALU.pow)

    memset("pool", NEGH.v(), -0.5)
    memset("pool", ONESF.v(), 1.0)
    identf = IDENTF
    memset("pool", identf.v(), 1.0)
    P.op("pool", lambda e: e.affine_select(out=identf.v().ap, in_=identf.v().ap, pattern=[[-1, 128]],
                                            compare_op=ALU.is_equal, fill=0.0, base=0, channel_multiplier=1),
         [identf.v()], [identf.v()])
    cp("dve", IDENT.v(), identf.v())
    memset("pool", CARRY[0].v(), 0.0)

    ccol = rcarve(512, 8, F32)
    P.dma(ccol.v(), ccol_d.v(), "c0")
    P.dma(COLS.v(0, 128, 32, 34), gq_d.v(), "c0")
    P.dma(COLS.v(0, 128, 34, 35), gkv_d.v(), "c0")
    P.dma(BSC.v(), bs_d.v(), "c0")
    P.dma(CONVW.v(), convw_d.v(), "c0")
    P.dma(CONVB.v(), convb_d.v(), "c0")
    gpre = rcarve(544, 16, F32)
    P.dma(gpre.v(0, 128, 0, 8), gpre1_d.v(), "c0")
    P.dma(gpre.v(0, 128, 8, 16), gpre2_d.v(), "c0")

    cact = rcarve(608, 8, F32)
    act(cact.v(), ccol.v(), AF.Silu)
    crep = rcarve(1024, 8 * 128, F32)
    for kc in range(8):
        ts("dve", crep.v(0, 128, kc * 128, (kc + 1) * 128), identf.v(), 0.0, cact.v(0, 128, kc, kc + 1), ALU.mult, ALU.add)
    ADA = KT.alias(F32)
    BADA = Sub(SB, KT.off + 6144 * 4, 6144, F32)
    P.dma(BADA.v(), View(bada_d.h[0:1, :].partition_broadcast(128), bada_d.v().reg), "c0")
    wst = [rcarve(8192 + i * 16384, 8 * 512, F32) for i in range(2)] + [Sub(SB, VA.off + i * 16384, 8 * 512, F32) for i in range(2)]
    for n in range(12):
        st_ = wst[n % 4]
        P.dma(st_.v().re("p (c n) -> p c n", c=8),
              View(wada_d.h[:, n * 512:(n + 1) * 512].rearrange("(c p) n -> p c n", p=128), wada_d.v(0, D, n * 512, (n + 1) * 512).reg),
              f"wa{n % 4}")
        pm = PM[n % 2]
        for kc in range(8):
            mm(pm.v(), crep.v(0, 128, kc * 128, (kc + 1) * 128), st_.v(0, 128, kc * 512, (kc + 1) * 512), kc == 0, kc == 7)
        tt("dve", ADA.v(0, 128, n * 512, (n + 1) * 512), pm.v(), BADA.v(0, 128, n * 512, (n + 1) * 512), ALU.add)

    memset("pool", VA.v(0, 128, 32 * 520, 32 * 520 + 64), 0.0)
    P.op("pool", lambda e: e.memset(VA.v(0, 128, 0, 32 * 520).ap.rearrange("p (k c) -> p k c", c=65)[:, :, 64:65], 1.0), [], [VA.v(0, 128, 0, 32 * 520)])
    tmpx = rcarve(8192, 1024, F32)
    colx = rcarve(640, 32, F32)
    for i, base in enumerate((0, 1024, 3072, 4096)):
        P.op("dve", lambda e, base=base: e.tensor_tensor(
            out=tmpx.v().ap.rearrange("p (j n) -> p j n", j=8),
            in0=ADA.v(0, 128, base, base + 1024).ap.rearrange("p (j n) -> p j n", j=8),
            in1=identf.v().ap.unsqueeze(1).to_broadcast([128, 8, 128]), op=ALU.mult),
            [ADA.v(0, 128, base, base + 1024), identf.v()], [tmpx.v()])
        red(colx.v(0, 128, i * 8, (i + 1) * 8), tmpx.v().re("p (j n) -> p j n", j=8))
    ts("dve", colx.v(0, 128, 8, 16), colx.v(0, 128, 8, 16), 1.0, None, ALU.add)
    ts("dve", colx.v(0, 128, 24, 32), colx.v(0, 128, 24, 32), 1.0, None, ALU.add)
    tt("dve", COLS.v(0, 128, 0, 8), colx.v(0, 128, 8, 16), gpre.v(0, 128, 0, 8), ALU.mult)
    tt("dve", COLS.v(0, 128, 16, 24), colx.v(0, 128, 24, 32), gpre.v(0, 128, 8, 16), ALU.mult)
    cp("dve", COLS.v(0, 128, 8, 16), colx.v(0, 128, 0, 8))
    cp("dve", COLS.v(0, 128, 24, 32), colx.v(0, 128, 16, 24))
    P.dma(GG1.v(), View(gpost1_d.h[0:1, :].partition_broadcast(128), gpost1_d.v().reg), "c0")
    P.dma(GG2.v(), View(gpost2_d.h[0:1, :].partition_broadcast(128), gpost2_d.v().reg), "c0")
    tt("dve", GG1.v(), GG1.v(), ADA.v(0, 128, 2048, 3072), ALU.mult)
    tt("dve", GG2.v(), GG2.v(), ADA.v(0, 128, 5120, 6144), ALU.mult)

    P.dma(GBC.v(), View(lng_d.h[0:1, :].partition_broadcast(128), lng_d.v().reg), "c0")
    P.dma(CST.v(), View(lnb_d.h[0:1, :].partition_broadcast(128), lnb_d.v().reg), "c0")

    wstage = [rcarve(8192 + i * 16384, 4096, F32) for i in range(2)]
    cast_engs = ["dve", "pool", "act"]
    cnt = [0]

    def stage_load(src_view_ap, src_reg, ncols):
        st_ = wstage[cnt[0] % 2]
        tagn = f"ws{cnt[0] % 2}"
        cnt[0] += 1
        P.dma(st_.v(0, 128, 0, ncols), View(src_view_ap, src_reg), tagn)
        return st_

    def cast(dst, src):
        eng = cast_engs[cnt[0] % 3]
        cp(eng, dst, src)

    for kc in range(8):
        st_ = stage_load(win_d.h[kc * 128:(kc + 1) * 128, :], win_d.v(kc * 128, (kc + 1) * 128).reg, 1440)
        b = kc * 1472
        cp("dve", WIN.v(0, 128, b, b + 416), st_.v(0, 128, 0, 416))
        cp("pool", WIN.v(0, 128, b + 416, b + 432), st_.v(0, 128, 400, 416))
        cp("pool", WIN.v(0, 128, b + 432, b + 448), st_.v(0, 128, 384, 400))
        cp("act", WIN.v(0, 128, b + 448, b + 1472), st_.v(0, 128, 416, 1440))
    for kc in range(2):
        st_ = stage_load(wuq_d.h[kc * 128:(kc + 1) * 128, :], wuq_d.v(kc * 128, (kc + 1) * 128).reg, 768)
        b = kc * 1024
        gq = COLS.v(0, 128, 32 + kc, 33 + kc)
        s3 = st_.v(0, 128, 0, 768).re("p (h c) -> p h c", h=8)
        P.op("dve", lambda e, b=b, s3=s3, gq=gq: e.tensor_scalar(
            out=WUQ.v(0, 128, b, b + 512).ap.rearrange("p (h c) -> p h c", h=8), in0=s3.ap[:, :, 0:64],
            scalar1=gq.ap, scalar2=None, op0=ALU.mult), [s3, gq], [WUQ.v(0, 128, b, b + 512)])
        P.op("dve", lambda e, b=b, s3=s3, gq=gq: e.tensor_scalar(
            out=WUQ.v(0, 128, b + 512, b + 768).ap.rearrange("p (h c) -> p h c", h=8), in0=s3.ap[:, :, 64:96],
            scalar1=gq.ap, scalar2=None, op0=ALU.mult), [s3, gq], [WUQ.v(0, 128, b + 512, b + 768)])
        P.op("dve", lambda e, b=b, s3=s3, gq=gq: e.tensor_scalar(
            out=WUQ.v(0, 128, b + 768, b + 1024).ap.rearrange("p (h c) -> p h c", h=8)[:, :, 0:16], in0=s3.ap[:, :, 80:96],
            scalar1=gq.ap, scalar2=None, op0=ALU.mult), [s3, gq], [WUQ.v(0, 128, b + 768, b + 1024)])
        P.op("dve", lambda e, b=b, s3=s3, gq=gq: e.tensor_scalar(
            out=WUQ.v(0, 128, b + 768, b + 1024).ap.rearrange("p (h c) -> p h c", h=8)[:, :, 16:32], in0=s3.ap[:, :, 64:80],
            scalar1=gq.ap, scalar2=None, op0=ALU.mult), [s3, gq], [WUQ.v(0, 128, b + 768, b + 1024)])
    st_ = stage_load(wukv_d.h[:, :], wukv_d.v().reg, 1024)
    gkv = COLS.v(0, 128, 34, 35)
    s3 = st_.v(0, 128, 0, 1024).re("p (h c) -> p h c", h=8)
    P.op("dve", lambda e, s3=s3: e.tensor_scalar(
        out=WUKV.v(0, 128, 0, 512).ap.rearrange("p (h c) -> p h c", h=8), in0=s3.ap[:, :, 0:64],
        scalar1=gkv.ap, scalar2=None, op0=ALU.mult), [s3, gkv], [WUKV.v(0, 128, 0, 512)])
    P.op("dve", lambda e, s3=s3: e.tensor_scalar(
        out=WUKV.v(0, 128, 512, 1024).ap.rearrange("p (h c) -> p h c", h=8), in0=s3.ap[:, :, 64:128],
        scalar1=gkv.ap, scalar2=None, op0=ALU.mult), [s3, gkv], [WUKV.v(0, 128, 512, 1024)])
    for kc in range(8):
        st_ = stage_load(wout_d.h[kc * 128:(kc + 1) * 128, :], wout_d.v(kc * 128, (kc + 1) * 128).reg, 1024)
        cast(WOUT.v(0, 128, kc * 1024, (kc + 1) * 1024), st_.v(0, 128, 0, 1024))
    st_ = stage_load(wsT_d.h[:, :], wsT_d.v().reg, 1024)
    P.op("pool", lambda e, st_=st_: e.memset(st_.v(64, 128, 0, 1024).ap.rearrange("p (h i) -> p h i", h=8)[:, :, 0:64], 0.0),
         [], [st_.v(64, 128, 0, 1024)])
    cp("dve", WST.v(), st_.v(0, 128, 0, 1024))
    onesb = rcarve(768, 8, BF16)
    memset("pool", onesb.v(), 1.0)
    rs = rcarve(800, 8, F32)
    for h in range(8):
        mm(PM[0].v(0, 128, h, h + 1), WST.v(0, 128, h * 128, (h + 1) * 128), onesb.v(0, 128, 0, 1), True, True)
    cp("dve", rs.v(), PM[0].v(0, 128, 0, 8))
    for h in range(8):
        ts("dve", CST.v(0, 128, h * 64, (h + 1) * 64), CST.v(0, 128, h * 64, (h + 1) * 64),
           rs.v(0, 128, h, h + 1), BSC.v(0, 128, h, h + 1), ALU.mult, ALU.add)

    HT = rcarve(0, 8 * 512, BF16)
    QT = rcarve(8192, 8 * 512, BF16)
    SGUT = rcarve(16384, 4 * 512, BF16)
    AT = rcarve(20480, 4 * 512, BF16)
    U0 = 24576
    XIN = [rcarve(U0 + 16384, 1024, F32), rcarve(U0 + 12288, 1024, F32)]
    XN = rcarve(0, 4096, BF16)
    CQF = rcarve(U0, 3 * 512, F32)
    SQ = rcarve(U0 + 6144, 2048, F32)
    DIAG = rcarve(U0 + 14336, 1024, F32)
    CQN = rcarve(U0 + 18432, 3 * 512, BF16)
    TMP1 = rcarve(U0, 512, F32)
    TMP2 = rcarve(U0 + 2048, 512, F32)
    ROT = rcarve(U0 + 4096, 512, BF16)
    CS = rcarve(U0 + 6144, 1024, F32)
    GBs = [rcarve(U0 + i * 10240, 1024, F32) for i in range(2)]
    SQ2s = [rcarve(U0 + i * 10240 + 4096, 512, F32) for i in range(2)]
    VNs = [rcarve(U0 + i * 10240 + 6144, 512, BF16) for i in range(2)]
    MIXs = [rcarve(U0 + i * 10240 + 7168, 512, F32) for i in range(2)]
    SGUs = [rcarve(U0 + i * 10240 + 9216, 512, BF16) for i in range(2)]
    PTBP = [rcarve(U0, 1024, BF16), rcarve(U0 + 2048, 1024, BF16), rcarve(U0 + 20480, 1024, BF16)]
    RDEN = [rcarve(U0 + 4096 + i * 2048, 512, F32) for i in range(2)]
    RDEN2 = [rcarve(U0 + 8192 + i * 2048, 512, F32) for i in range(2)]
    XRES = [rcarve(U0 + i * 4096, 1024, F32) for i in range(2)]
    TT_ = [rcarve(U0 + 8192 + i * 4096, 1024, F32) for i in range(2)]

    def tbank(fc, banks=None):
        banks = banks or [PT[0], PT[1], PM[0].as_bf16(), PM[1].as_bf16()]
        pb = banks[fc // 2]
        return lambda c0, c1: pb.v(0, 128, c0, c1)

    def prep_a(src_d, t0, XINb, XNb):
        for sbk in range(4):
            r0 = t0 + sbk * 128
            xin = XINb[sbk % 2]
            xn = XNb.v(0, 128, sbk * 1024, (sbk + 1) * 1024)
            P.dma(xin.v(), src_d.v(r0, r0 + 128), f"xin{sbk % 2}")
            act(xn, xin.v(), AF.Square, accum=ST.v(0, 128, 2 * sbk, 2 * sbk + 1))
            rstd_from(ST.v(0, 128, 2 * sbk + 1, 2 * sbk + 2), ST.v(0, 128, 2 * sbk, 2 * sbk + 1), D, 1)
            ts("dve", xn, xin.v(), ST.v(0, 128, 2 * sbk + 1, 2 * sbk + 2), None, ALU.mult)

    def prep_b(gmod_c, sh_c, HTbuf, XNb, banks=None):
        for fc in range(8):
            bank = tbank(fc, banks)
            for sbk in range(4):
                c0 = (fc % 2) * 512 + sbk * 128
                tr(bank(c0, c0 + 128), XNb.v(0, 128, sbk * 1024 + fc * 128, sbk * 1024 + (fc + 1) * 128))
        for fc in range(8):
            bank = tbank(fc, banks)
            c0 = (fc % 2) * 512
            act(HTbuf.v(0, 128, fc * 512, (fc + 1) * 512), bank(c0, c0 + 512), AF.Identity,
                scale=COLS.v(0, 128, gmod_c + fc, gmod_c + fc + 1), bias=COLS.v(0, 128, sh_c + fc, sh_c + fc + 1))

    def prep_hT(src_d, t0, gmod_c, sh_c, HTbuf, XINb, XNb):
        prep_a(src_d, t0, XINb, XNb)
        prep_b(gmod_c, sh_c, HTbuf, XNb)

    def post_norm_residual(pm_pair, src_d, dst_d, r0, GG, slot, XRESb, TTb):
        xr = XRESb[slot]
        tb = TTb[slot]
        tbj = tb.alias(BF16)
        s0 = 8 + 4 * slot
        P.dma(xr.v(), src_d.v(r0, r0 + 128), f"xr{slot}")
        for hf in range(2):
            act(tbj.v(0, 128, hf * 512, (hf + 1) * 512), pm_pair[hf].v(), AF.Square, accum=ST.v(0, 128, s0 + hf, s0 + hf + 1))
        tt("dve", ST.v(0, 128, s0 + 2, s0 + 3), ST.v(0, 128, s0, s0 + 1), ST.v(0, 128, s0 + 1, s0 + 2), ALU.add)
        rstd_from(ST.v(0, 128, s0 + 3, s0 + 4), ST.v(0, 128, s0 + 2, s0 + 3), D, 1)
        for hf in range(2):
            tt("dve", tb.v(0, 128, hf * 512, (hf + 1) * 512), pm_pair[hf].v(), GG.v(0, 128, hf * 512, (hf + 1) * 512), ALU.mult)
        stt(xr.v(), tb.v(), ST.v(0, 128, s0 + 3, s0 + 4), xr.v(), ALU.mult, ALU.add)
        P.dma(dst_d.v(r0, r0 + 128), xr.v(), f"xo{slot}")

    for T in range(NT):
        t0 = T * 512
        if T == 0:
            prep_a(x_d, t0, XIN, XN)
        if T == 0:
            prep_b(0, 8, HT, XN)
        for c in range(3):
            pm = PM[c % 2]
            for kc in range(8):
                mm(pm.v(), WIN.v(0, 128, kc * 1472 + c * 128, kc * 1472 + (c + 1) * 128), HT.v(0, 128, kc * 512, (kc + 1) * 512), kc == 0, kc == 7)
            cp("act", CQF.v(0, 128, c * 512, (c + 1) * 512), pm.v())
        def norm_steps(grp, chunks, nfeat):
            pss = PSS[grp]
            pbc = PM[grp]
            sc0 = 40 + grp * 4

            def squares():
                for i, c in enumerate(chunks):
                    act(SQ.v(0, 128, (grp * 2 + i) * 512, (grp * 2 + i + 1) * 512), CQF.v(0, 128, c * 512, (c + 1) * 512), AF.Square)

            def sums():
                for sbk in range(4):
                    for i, c in enumerate(chunks):
                        b = (grp * 2 + i) * 512 + sbk * 128
                        mm(pss.v(0, 128, sbk, sbk + 1), SQ.v(0, 128, b, b + 128), ONESF.v(0, 128, 0, 1), i == 0, i == len(chunks) - 1)

            def bcast():
                for sbk in range(4):
                    ts("dve", DIAG.v(0, 128, grp * 512 + sbk * 128, grp * 512 + (sbk + 1) * 128), IDENTF.v(), ST.v(0, 128, sc0 + sbk, sc0 + sbk + 1), None, ALU.mult)
                    mm(pbc.v(0, 128, sbk * 128, (sbk + 1) * 128), ONESF.v(), DIAG.v(0, 128, grp * 512 + sbk * 128, grp * 512 + (sbk + 1) * 128), True, True)

            def apply():
                for c in chunks:
                    tt("dve", CQN.v(0, 128, c * 512, (c + 1) * 512), CQF.v(0, 128, c * 512, (c + 1) * 512), pbc.v(), ALU.mult)

            return [
                squares,
                sums,
                lambda: ts("dve", ST.v(0, 128, sc0, sc0 + 4), pss.v(0, 128, 0, 4), 1.0 / nfeat, EPS, ALU.mult, ALU.add),
                lambda: tt("pool", ST.v(0, 128, sc0, sc0 + 4), ST.v(0, 128, sc0, sc0 + 4), NEGH.v(0, 128, 0, 4), ALU.pow),
                bcast,
                apply,
            ]

        for fa, fb in zip(norm_steps(0, (0, 1), 256), norm_steps(1, (2,), 128)):
            fa()
            fb()
        P.dma(CS.v(0, 128, 0, 512), cos_d.v(0, 128, t0, t0 + 512), "cs")
        P.dma(CS.v(0, 128, 512, 1024), sin_d.v(0, 128, t0, t0 + 512), "cs")
        for sw in range(2):
            pm = PM[sw]
            for kc in range(8):
                b = kc * 1472 + 384 + sw * 32
                mm(pm.v(64, 96), WIN.v(0, 128, b, b + 32), HT.v(0, 128, kc * 512, (kc + 1) * 512), kc == 0, kc == 7)
        tt("dve", TMP1.v(64, 96), PM[0].v(64, 96), CS.v(64, 96, 0, 512), ALU.mult)
        tt("dve", TMP2.v(64, 96), PM[1].v(64, 96), CS.v(64, 96, 512, 1024), ALU.mult)
        tt("pool", ROT.v(64, 96), TMP1.v(64, 96), TMP2.v(64, 96), ALU.add)
        for h in range(8):
            cp("dve", KT.v(64, 96, h * 4096 + t0, h * 4096 + t0 + 512), ROT.v(64, 96))
        for g in range(2):
            pr, psw = PSS[0], PSS[1]
            for kc in range(2):
                b = kc * 1024 + 512 + g * 128
                mm(pr.v(), WUQ.v(0, 128, b, b + 128), CQN.v(0, 128, kc * 512, (kc + 1) * 512), kc == 0, kc == 1)
            for kc in range(2):
                b = kc * 1024 + 768 + g * 128
                mm(psw.v(), WUQ.v(0, 128, b, b + 128), CQN.v(0, 128, kc * 512, (kc + 1) * 512), kc == 0, kc == 1)
            tt("dve", TMP1.v(), pr.v(), CS.v(0, 128, 0, 512), ALU.mult)
            tt("dve", TMP2.v(), psw.v(), CS.v(0, 128, 512, 1024), ALU.mult)
            tt("pool", ROT.v(), TMP1.v(), TMP2.v(), ALU.add)
            for hh in range(4):
                h = g * 4 + hh
                cp(("dve", "act")[hh % 2], QT.v(64, 96, h * 512, (h + 1) * 512), ROT.v(32 * hh, 32 * hh + 32))
            for hp in range(2 * g, 2 * g + 2):
                pa = PM[hp % 2]
                for kc in range(2):
                    b = kc * 1024 + hp * 128
                    mm(pa.v(), WUQ.v(0, 128, b, b + 128), CQN.v(0, 128, kc * 512, (kc + 1) * 512), kc == 0, kc == 1)
                cp("act", QT.v(0, 64, (2 * hp) * 512, (2 * hp + 1) * 512), pa.v(0, 64))
                cp("dve", QT.v(0, 64, (2 * hp + 1) * 512, (2 * hp + 2) * 512), pa.v(64, 128))
        for h in range(8):
            pk = PO[h % 2]
            mm(pk.v(0, 64), WUKV.v(0, 128, h * 64, (h + 1) * 64), CQN.v(0, 128, 1024, 1536), True, True)
            cp(("act", "dve")[h % 2], KT.v(0, 64, h * 4096 + t0, h * 4096 + t0 + 512), pk.v(0, 64))
        for sbk in range(4):
            kt = T * 4 + sbk
            pv = PO[sbk % 2]
            mm(pv.v(), CQN.v(0, 128, 1024 + sbk * 128, 1024 + (sbk + 1) * 128), WUKV.v(0, 128, 512, 1024), True, True)
            P.op("dve", lambda e, kt=kt, pv=pv: e.tensor_copy(
                out=VA.v(0, 128, kt * 520, (kt + 1) * 520).ap.rearrange("p (h c) -> p h c", c=65)[:, :, 0:64],
                in_=pv.v().ap.rearrange("p (h c) -> p h c", c=64)), [pv.v()], [VA.v(0, 128, kt * 520, (kt + 1) * 520)])
        def gm_Gmm(sbk):
            i2 = sbk % 2
            gbank = PM if i2 == 0 else PO
            for hf in range(2):
                pm = gbank[hf]
                for kc in range(8):
                    b = kc * 1472 + 448 + hf * 512
                    mm(pm.v(), HT.v(0, 128, kc * 512 + sbk * 128, kc * 512 + (sbk + 1) * 128), WIN.v(0, 128, b, b + 512), kc == 0, kc == 7)

        def gm_Gact(sbk):
            i2 = sbk % 2
            gbank = PM if i2 == 0 else PO
            for hf in range(2):
                act(GBs[i2].v(0, 128, hf * 512, (hf + 1) * 512), gbank[hf].v(), AF.Gelu_apprx_tanh)

        def gm_S(sbk):
            i2 = sbk % 2
            GB, SQ2, VN, MIX, SGU = GBs[i2], SQ2s[i2], VNs[i2], MIXs[i2], SGUs[i2]
            sb0 = 16 if i2 == 0 else 64
            sA, sB, sC = sb0, sb0 + 8, sb0 + 16
            vv = GB.v(0, 128, 512, 1024)
            uu = GB.v(0, 128, 0, 512)
            h3 = lambda v: v.re("p (h d) -> p h d", h=8)
            mean_b = View(ST.v(0, 128, sA, sA + 8).ap.unsqueeze(2).to_broadcast([128, 8, 64]), ST.v(0, 128, sA, sA + 8).reg)
            rstd_b = View(ST.v(0, 128, sB, sB + 8).ap.unsqueeze(2).to_broadcast([128, 8, 64]), ST.v(0, 128, sB, sB + 8).reg)
            pm = PSS[sbk % 2]
            pt = PT[sbk % 2]

            def spatial():
                for h in range(8):
                    mm(pm.v(0, 128, h * 64, (h + 1) * 64), WST.v(0, 128, h * 128, (h + 1) * 128), VN.v(0, 128, h * 64, (h + 1) * 64), True, True)

            def transposes():
                for c in range(4):
                    tr(pt.v(0, 128, c * 128, (c + 1) * 128), SGU.v(0, 128, c * 128, (c + 1) * 128))

            return [
                lambda: red(ST.v(0, 128, sA, sA + 8), h3(vv)),
                lambda: act(SQ2.v(), vv, AF.Square),
                lambda: red(ST.v(0, 128, sB, sB + 8), h3(SQ2.v())),
                lambda: ts("dve", ST.v(0, 128, sA, sA + 8), ST.v(0, 128, sA, sA + 8), 1.0 / 64, None, ALU.mult),
                lambda: tt("dve", ST.v(0, 128, sC, sC + 8), ST.v(0, 128, sA, sA + 8), ST.v(0, 128, sA, sA + 8), ALU.mult),
                lambda: stt(ST.v(0, 128, sB, sB + 8), ST.v(0, 128, sB, sB + 8), 1.0 / 64, ST.v(0, 128, sC, sC + 8), ALU.mult, ALU.subtract),
                lambda: ts("dve", ST.v(0, 128, sB, sB + 8), ST.v(0, 128, sB, sB + 8), EPS, None, ALU.add),
                lambda: tt("pool", ST.v(0, 128, sB, sB + 8), ST.v(0, 128, sB, sB + 8), NEGH.v(0, 128, 0, 8), ALU.pow),
                lambda: tt("dve", h3(SQ2.v()), h3(vv), mean_b, ALU.subtract),
                lambda: tt("dve", h3(VN.v()), h3(SQ2.v()), rstd_b, ALU.mult),
                spatial,
                lambda: tt("dve", MIX.v(), pm.v(), GBC.v(), ALU.mult),
                lambda: tt("dve", MIX.v(), MIX.v(), CST.v(), ALU.add),
                lambda: tt("pool", SGU.v(), MIX.v(), uu, ALU.mult),
                transposes,
                lambda: P.op("act", lambda e: e.activation(
                    out=SGUT.v().ap.rearrange("p (c t) -> p c t", c=4)[:, :, sbk * 128:(sbk + 1) * 128],
                    in_=pt.v(0, 128, 0, 512).ap.rearrange("p (c t) -> p c t", c=4), func=AF.Copy),
                    [pt.v()], [SGUT.v()]),
            ]

        def lockstep(a, b):
            for fa, fb in zip(a, b):
                fa()
                fb()

        gm_Gmm(0)
        gm_Gact(0)
        gm_Gmm(1)
        gm_Gact(1)
        gm_Gmm(2)
        gm_Gmm(3)
        lockstep(gm_S(0), gm_S(1))
        gm_Gact(2)
        gm_Gact(3)
        lockstep(gm_S(2), gm_S(3))

        nkt = 4 * T + 4
        pairs = [(h, j) for h in range(8) for j in range(nkt // 2)]
        SROOT = [RS_, RT_]

        def geom(kt):
            r = kt - 4 * T
            q0 = 128 * r if r > 0 else 0
            return r, q0, 512 - q0

        def qk(g):
            h, j = pairs[g]
            for e in range(2):
                kt = 2 * j + e
                r, q0, n = geom(kt)
                mm(PB(SROOT[g % 2], e).v(0, 128, 0, n), KT.v(0, 96, h * 4096 + kt * 128, h * 4096 + (kt + 1) * 128),
                   QT.v(0, 96, h * 512 + q0, (h + 1) * 512), True, True)

        def do_exp(g):
            h, j = pairs[g]
            sr = SROOT[g % 2]
            ptb = PTBP[g % 3]
            ge = [geom(2 * j + e) for e in range(2)]
            if ge[0][2] == 512 and ge[1][2] == 512:
                act(ptb.v(0, 128, 0, 1024), View(sr.h[:, 0:1024], (sr.id, 0, 128, 0, 1024)), AF.Exp, scale=ATTN_SCALE)
            else:
                for e in range(2):
                    n = ge[e][2]
                    act(ptb.v(0, 128, e * 512, e * 512 + n), PB(sr, e).v(0, 128, 0, n), AF.Exp, scale=ATTN_SCALE)
            for e in range(2):
                if ge[e][0] >= 0:
                    memset("pool", ptb.v(64, 128, e * 512, e * 512 + 64), 0.0)

        def do_pv(g):
            h, j = pairs[g]
            ptb = PTBP[g % 3]
            po = PO[h % 2]
            for e in range(2):
                kt = 2 * j + e
                r, q0, n = geom(kt)
                voff = kt * 520 + h * 65
                mm(po.v(0, 128, q0, 512), VA.v(0, 128, voff, voff + 128), ptb.v(0, 128, e * 512, e * 512 + n), kt == 0, kt == nkt - 1)
            if h == 1 and j == nkt // 2 - 1 and T + 1 < NT:
                prep_a(x_d, t0 + 512, XIN, XN)
            if j == nkt // 2 - 1:
                rd_ = RDEN[h % 2]
                c, half = h // 2, h % 2
                p0 = half * 64
                recip(rd_.v(64, 65), po.v(64, 65))
                pb = PM[h % 2]
                mm(pb.v(p0, p0 + 64), ONESF.v(64, 65, 0, 64), rd_.v(64, 65), True, True)
                bc = rd_.v(0, 64) if p0 == 0 else RDEN2[h % 2].v(64, 128)
                cp("act", bc, pb.v(p0, p0 + 64))
                tt("dve", AT.v(p0, p0 + 64, c * 512, (c + 1) * 512), po.v(0, 64), bc, ALU.mult)

        qk(0)
        for g in range(len(pairs) + 1):
            if g + 1 < len(pairs):
                qk(g + 1)
            if g < len(pairs):
                do_exp(g)
            if g >= 1:
                do_pv(g - 1)

        for sbk in range(4):
            pair = (PM[0], PM[1]) if sbk % 2 == 0 else (PSS[0], PSS[1])
            for hf in range(2):
                for kc in range(8):
                    src = AT if kc < 4 else SGUT
                    cc = kc % 4
                    mm(pair[hf].v(), src.v(0, 128, cc * 512 + sbk * 128, cc * 512 + (sbk + 1) * 128),
                       WOUT.v(0, 128, kc * 1024 + hf * 512, kc * 1024 + (hf + 1) * 512), kc == 0, kc == 7)
            post_norm_residual(pair, x_d, out_d, t0 + sbk * 128, GG1, sbk % 2, XRES, TT_)
            if sbk == 1 and T + 1 < NT:
                prep_b(0, 8, HT, XN, [PT[0], PT[1], PO[0].as_bf16(), PO[1].as_bf16()])

    if not phase2:
        P.finalize()
        P.emit()
        return nc

    H2T = rcarve(0, 8 * 512, BF16)
    ACTT = rcarve(8192, 22 * 512, BF16)
    XRES2 = [rcarve(30720 + i * 4096, 1024, F32) for i in range(2)]
    TT2 = [rcarve(38912 + i * 4096, 1024, F32) for i in range(2)]
    XIN2 = [Sub(SB, PH2_END + i * 4096, 1024, F32) for i in range(2)]
    XN2 = rcarve(0, 4096, BF16)
    YAB = [[Sub(SB, PH2_END + (ab * 2 + i) * 2048, 512, F32) for i in range(2)] for ab in range(2)]
    assert PH2_END + 8192 <= IDENT.off, (PH2_END, IDENT.off)
    st2 = [rcarve(30720 + i * 4096, 1024, F32) for i in range(3)]
    k2 = [0]

    def load_wdn(c):
        s_ = st2[k2[0] % 3]
        P.dma(s_.v(), wdn_d.v(c * 128, (c + 1) * 128), f"w2{k2[0] % 3}")
        cp(("dve", "act")[k2[0] % 2], WDN.v(0, 128, c * 1024, (c + 1) * 1024), s_.v())
        k2[0] += 1

    def load_wup(c, ab):
        s_ = st2[k2[0] % 3]
        col = ab * DFF + c * 128
        P.dma(s_.v().re("p (k n) -> p k n", k=8),
              View(wup_d.h[:, col:col + 128].rearrange("(k p) n -> p k n", p=128), wup_d.v(0, D, col, col + 128).reg),
              f"w2{k2[0] % 3}")
        base = WUP.off // 2 + col
        dst = View(SB.h[:, base:base + 8 * 5632].rearrange("p (k n) -> p k n", n=5632)[:, :, 0:128],
                   WUP.v(0, 128, col, col + 128).reg).also(*[WUP.v(0, 128, kc * 5632 + col, kc * 5632 + col + 128).reg for kc in range(1, 8)])
        eng = ("act", "pool")[k2[0] % 2]
        src = s_.v().re("p (k n) -> p k n", k=8)
        if eng == "act":
            P.op("act", lambda e: e.activation(out=dst.ap, in_=src.ap, func=AF.Copy), [src], [dst])
        else:
            P.op("pool", lambda e: e.tensor_copy(out=dst.ap, in_=src.ap), [src], [dst])
        k2[0] += 1

    prep_a(out_d, 0, XIN2, XN2)
    prep_b(16, 24, H2T, XN2)
    for c in range(NCH):
        load_wdn(c)
    LA = 2
    RSM0, RSM1 = PM[0].as_bf16(), PM[1].as_bf16()

    for T in range(NT):
        t0 = T * 512
        cold = CARRY[T % 2]
        cnew = CARRY[(T + 1) % 2]
        if T == 0:
            for c in range(LA):
                load_wup(c, 0)
                load_wup(c, 1)
        for c in range(NCH):
            if T == 0 and c + LA < NCH:
                load_wup(c + LA, 0)
                load_wup(c + LA, 1)
            ys = []
            for ab in range(2):
                pm = (PM, PSS, PO)[c % 3][ab]
                col = ab * DFF + c * 128
                for kc in range(8):
                    mm(pm.v(), WUP.v(0, 128, kc * 5632 + col, kc * 5632 + col + 128), H2T.v(0, 128, kc * 512, (kc + 1) * 512), kc == 0, kc == 7)
                ci = ab * NCH + c
                y = YAB[ab][c % 2]
                ys.append(y)
                w = lambda kk, ci=ci: CONVW.v(0, 128, ci * 3 + kk, ci * 3 + kk + 1)
                act(y.v(), pm.v(), AF.Identity, scale=w(2), bias=CONVB.v(0, 128, ci, ci + 1))
                stt(y.v(0, 128, 1, 512), pm.v(0, 128, 0, 511), w(1), y.v(0, 128, 1, 512), ALU.mult, ALU.add)
                stt(y.v(0, 128, 2, 512), pm.v(0, 128, 0, 510), w(0), y.v(0, 128, 2, 512), ALU.mult, ALU.add)
                stt(y.v(0, 128, 0, 2), cold.v(0, 128, ci * 2, ci * 2 + 2), w(0), y.v(0, 128, 0, 2), ALU.mult, ALU.add)
                stt(y.v(0, 128, 0, 1), cold.v(0, 128, ci * 2 + 1, ci * 2 + 2), w(1), y.v(0, 128, 0, 1), ALU.mult, ALU.add)
                cp("act", cnew.v(0, 128, ci * 2, ci * 2 + 2), pm.v(0, 128, 510, 512))
            act(ys[0].v(), ys[0].v(), AF.Silu)
            tt("pool", ACTT.v(0, 128, c * 512, (c + 1) * 512), ys[0].v(), ys[1].v(), ALU.mult)
        if T + 1 < NT:
            prep_a(out_d, t0 + 512, XIN2, XN2)
        for sbk in range(4):
            pair = (PO[0], PO[1]) if sbk % 2 == 0 else (PSS[0], PSS[1])
            for hf in range(2):
                for c in range(NCH):
                    mm(pair[hf].v(), ACTT.v(0, 128, c * 512 + sbk * 128, c * 512 + (sbk + 1) * 128),
                       WDN.v(0, 128, c * 1024 + hf * 512, c * 1024 + (hf + 1) * 512), c == 0, c == NCH - 1)
            post_norm_residual(pair, out_d, out_d, t0 + sbk * 128, GG2, sbk % 2, XRES2, TT2)
            if sbk == 1 and T + 1 < NT:
                prep_b(16, 24, H2T, XN2, [PT[0], PT[1], RSM0, RSM1])

    P.finalize()
    P.emit()
    return nc


def _rope_tables():
    pos = np.arange(S, dtype=np.float32)
    inv = (np.float32(10000.0) ** (-np.arange(0, 32, 2, dtype=np.float32) / np.float32(32))).astype(np.float32)
    ang = pos[None, :] * inv[:, None]
    cos = np.cos(ang).astype(np.float32)
    sin = np.sin(ang).astype(np.float32)
    cosT = np.concatenate([cos, cos], 0)
    sinT = np.concatenate([-sin, sin], 0)
    return np.tile(cosT, (4, 1)).copy(), np.tile(sinT, (4, 1)).copy()


def _col(v, n):
    return np.ascontiguousarray(np.asarray(v, np.float32).reshape(n, 128).T)


def make_in_maps(inputs):
    f = lambda a: np.ascontiguousarray(np.asarray(a, np.float32))
    cosT, sinT = _rope_tables()
    shared = {
        "w_ada": f(inputs["w_ada"][0]), "b_ada": f(inputs["b_ada"][0]).reshape(1, -1),
        "gpre1": _col(inputs["g_pre_mix"][0], 8), "gpre2": _col(inputs["g_pre_ffn"][0], 8),
        "gpost1": f(inputs["g_post_mix"][0]).reshape(1, -1), "gpost2": f(inputs["g_post_ffn"][0]).reshape(1, -1),
        "w_in": f(inputs["w_in"][0]), "gq": _col(inputs["g_q"][0], 2), "w_uq": f(inputs["w_uq"][0]),
        "gkv": _col(inputs["g_kv"][0], 1), "w_ukv": f(inputs["w_ukv"][0]),
        "lng": f(inputs["gm_ln_g"][0]).reshape(1, 512), "lnb": f(inputs["gm_ln_b"][0]).reshape(1, 512),
        "wsT": np.ascontiguousarray(np.transpose(f(inputs["w_spatial"][0]), (2, 0, 1)).reshape(128, 1024)),
        "bs": np.ascontiguousarray(f(inputs["b_spatial"][0]).T),
        "w_out": f(inputs["w_out"][0]), "w_up": f(inputs["w_up"][0]),
        "convw": np.ascontiguousarray(np.transpose(f(inputs["conv_w"][0]).reshape(3, 44, 128), (2, 1, 0)).reshape(128, 132)),
        "convb": np.ascontiguousarray(f(inputs["conv_b"][0]).reshape(44, 128).T),
        "w_down": f(inputs["w_down"][0]), "cosT": cosT, "sinT": sinT,
    }
    x = f(inputs["x"])
    c = f(inputs["c"])
    maps = []
    for b in range(N_CORES):
        m = dict(shared)
        m["x"] = x[b]
        m["ccol"] = _col(c[b], 8)
        maps.append(m)
    return maps


_NC_CACHE = {}


def kernel(**inputs):
    if "nc" not in _NC_CACHE:
        _NC_CACHE["nc"] = build(8, True)
    nc = _NC_CACHE["nc"]
    in_maps = make_in_maps(inputs)
    res = run_bass_kernel_spmd(nc, in_maps, core_ids=list(range(N_CORES)))
    return np.stack([np.asarray(r["out"], np.float32) for r in res.results], 0)
```

```python
import bisect
import numpy as np
import concourse.bass as bass
import concourse.mybir as mybir
from concourse.bass_utils import run_bass_kernel_spmd

F32 = mybir.dt.float32
F32R = mybir.dt.float32r
BF16 = mybir.dt.bfloat16
ALU = mybir.AluOpType
AF = mybir.ActivationFunctionType
AX = mybir.AxisListType

ENGS = ("pe", "act", "dve", "pool", "sp")
DTSZ = {F32: 4, F32R: 4, BF16: 2}

S = 4096
D = 1024
NHEAD = 8
DFF = 2816
NCH = 22
ATTN_SCALE = 96.0 ** -0.5
EPS = 1e-6
N_CORES = 8


class View:
    __slots__ = ("ap", "reg", "extra")

    def __init__(self, ap, reg):
        self.ap = ap
        self.reg = reg
        self.extra = ()

    def re(self, pat, **kw):
        return View(self.ap.rearrange(pat, **kw), self.reg)

    def also(self, *regs):
        v = View(self.ap, self.reg)
        v.extra = list(regs)
        return v


class Root:
    def __init__(self, prog, h, P, F, kind):
        self.prog = prog
        self.h = h
        self.P = P
        self.F = F
        self.kind = kind
        self.id = len(prog.roots)
        prog.roots.append(self)
        self.recs = []

    def v(self, p0=0, p1=None, f0=0, f1=None):
        p1 = self.P if p1 is None else p1
        f1 = self.F if f1 is None else f1
        assert 0 <= p0 < p1 <= self.P and 0 <= f0 < f1 <= self.F, (p0, p1, f0, f1)
        if self.kind == "ps":
            reg = (self.id, 0, self.P, (f0 // 512) * 512, ((f1 + 511) // 512) * 512)
        else:
            reg = (self.id, p0, p1, f0, f1)
        return View(self.h[p0:p1, f0:f1], reg)


class Sub:
    def __init__(self, parent, off, F, dt):
        assert off % 4 == 0
        self.parent = parent
        self.off = off
        self.F = F
        self.dt = dt
        self.sz = DTSZ[dt]
        assert off + F * self.sz <= parent.F * 2, (off, F, self.sz, parent.F * 2)

    def v(self, p0=0, p1=128, f0=0, f1=None):
        f1 = self.F if f1 is None else f1
        assert 0 <= p0 < p1 <= 128 and 0 <= f0 < f1 <= self.F, (p0, p1, f0, f1, self.F)
        b0 = self.off + f0 * self.sz
        b1 = self.off + f1 * self.sz
        assert b0 % 2 == 0 and b1 % 2 == 0
        ap = self.parent.h[p0:p1, b0 // 2:b1 // 2]
        if self.dt != BF16:
            ap = ap.bitcast(self.dt)
        return View(ap, (self.parent.id, p0, p1, b0 // 2, b1 // 2))

    def alias(self, dt):
        return Sub(self.parent, self.off, self.F * self.sz // DTSZ[dt], dt)


class PB:
    def __init__(self, root, bank, dt=None):
        self.root = root
        self.bank = bank
        self.dt = dt or F32
        self.F = 512 if self.dt == F32 else 1024

    def v(self, p0=0, p1=128, f0=0, f1=None):
        f1 = self.F if f1 is None else f1
        base = self.root.h[p0:p1, self.bank * 512:(self.bank + 1) * 512]
        if self.dt == BF16:
            base = base.bitcast(BF16)
        return View(base[:, f0:f1], (self.root.id, 0, 128, self.bank * 512, (self.bank + 1) * 512))

    def as_bf16(self):
        return PB(self.root, self.bank, BF16)


class Op:
    __slots__ = ("eng", "fn", "reads", "writes", "tag", "idx", "deps", "inc", "semval", "waits")


class Prog:
    def __init__(self, nc):
        self.nc = nc
        self.roots = []
        self.ops = []

    def sbuf_root(self, name, ncols):
        h = self.nc.alloc_sbuf_tensor(name, [128, ncols], BF16)
        return Root(self, h, 128, ncols, "sb")

    def ps(self, name, F, dt=F32):
        h = self.nc.alloc_psum_tensor(name, [128, F], dt)
        return Root(self, h, 128, F, "ps")

    def dram(self, name, shape, dt, kind):
        h = self.nc.dram_tensor(name, list(shape), dt, kind=kind)
        R = shape[0]
        C = int(np.prod(shape[1:])) if len(shape) > 1 else 1
        return Root(self, h, R, C, "dram")

    def op(self, eng, fn, reads=(), writes=(), tag=None):
        o = Op()
        o.eng = eng
        o.fn = fn
        o.reads = [r.reg for r in reads] + [x for r in reads for x in r.extra]
        o.writes = [w.reg for w in writes] + [x for w in writes for x in w.extra]
        o.tag = tag
        o.idx = len(self.ops)
        o.deps = set()
        o.inc = False
        o.semval = 0
        o.waits = []
        self.ops.append(o)
        self._track(o)
        return o

    def dma(self, out, in_, tag, **kw):
        return self.op("sp", lambda e: e.dma_start(out=out.ap, in_=in_.ap, **kw), [in_], [out], tag=tag)

    def _track(self, o):
        ops = self.ops
        roots = self.roots
        for reg in o.reads:
            r0, r1, r2, r3 = reg[1:]
            for rec in roots[reg[0]].recs:
                if rec[5] and rec[0] < r1 and r0 < rec[1] and rec[2] < r3 and r2 < rec[3]:
                    o.deps.add(rec[4])
        for reg in o.writes:
            root = roots[reg[0]]
            r0, r1, r2, r3 = reg[1:]
            keep = []
            for rec in root.recs:
                if rec[0] < r1 and r0 < rec[1] and rec[2] < r3 and r2 < rec[3]:
                    if rec[4] != o.idx:
                        o.deps.add(rec[4])
                    if r0 <= rec[0] and rec[1] <= r1 and r2 <= rec[2] and rec[3] <= r3:
                        continue
                keep.append(rec)
            root.recs = keep
        for reg in o.reads:
            root = roots[reg[0]]
            r = reg[1:]
            if o.eng != "sp":
                root.recs = [
                    rec for rec in root.recs
                    if rec[5] or rec[:4] != r or ops[rec[4]].eng != o.eng
                ]
            root.recs.append((r[0], r[1], r[2], r[3], o.idx, False))
        for reg in o.writes:
            r = reg[1:]
            roots[reg[0]].recs.append((r[0], r[1], r[2], r[3], o.idx, True))

    def finalize(self):
        ops = self.ops
        per_eng = {e: [] for e in ENGS}
        for o in ops:
            per_eng[o.eng].append(o)
        needed = []
        for o in ops:
            best = {}
            for d in o.deps:
                p = ops[d]
                if p.eng == "pe" and o.eng == "pe":
                    continue
                key = ("tag", p.tag) if p.eng == "sp" else ("eng", p.eng)
                if key not in best or best[key] < p.idx:
                    best[key] = p.idx
            needed.append(best)
            for pidx in best.values():
                ops[pidx].inc = True
        for e in ("pe", "act", "dve", "pool"):
            c = 0
            for o in per_eng[e]:
                if o.inc:
                    c += 1
                    o.semval = c
        tag_lists = {}
        for o in per_eng["sp"]:
            tag_lists.setdefault(o.tag, []).append(o.idx)
        self.tags = sorted(tag_lists.keys())
        waited = {e: {} for e in ENGS}
        for o in ops:
            w = []
            for key, pidx in needed[o.idx].items():
                if key[0] == "eng":
                    val = ops[pidx].semval
                else:
                    val = 16 * bisect.bisect_left(tag_lists[key[1]], o.idx)
                if waited[o.eng].get(key, 0) >= val:
                    continue
                waited[o.eng][key] = val
                w.append((key, val))
            o.waits = w
        self.per_eng = per_eng
        self.tag_totals = {t: len(l) for t, l in tag_lists.items()}

    def emit(self):
        nc = self.nc
        sems = {}
        for e in ("pe", "act", "dve", "pool"):
            sems[("eng", e)] = nc.alloc_semaphore(name=f"s_{e}")
        for t in self.tags:
            sems[("tag", t)] = nc.alloc_semaphore(name=f"d_{t}")
        per_eng = self.per_eng
        totals = self.tag_totals

        def run(eng_name, e):
            for o in per_eng[eng_name]:
                for key, val in o.waits:
                    e.wait_ge(sems[key], val)
                ins = o.fn(e)
                if eng_name == "sp":
                    ins.then_inc(sems[("tag", o.tag)], 16)
                elif o.inc:
                    ins.then_inc(sems[("eng", eng_name)], 1)
            if eng_name == "sp":
                for t, c in totals.items():
                    e.wait_ge(sems[("tag", t)], 16 * c)

        with nc.Block() as block:
            @block.sync
            def _(e):
                run("sp", e)

            @block.tensor
            def _(e):
                run("pe", e)

            @block.scalar
            def _(e):
                run("act", e)

            @block.vector
            def _(e):
                run("dve", e)

            @block.gpsimd
            def _(e):
                run("pool", e)


def build(NT=8, phase2=True):
    nc = bass.Bass("TRN2", target_bir_lowering=False)
    P = Prog(nc)
    SEQ = NT * 512

    def din(name, shape):
        return P.dram(name, shape, F32, "ExternalInput")

    x_d = din("x", [S, D])
    ccol_d = din("ccol", [128, 8])
    wada_d = din("w_ada", [D, 6 * D])
    bada_d = din("b_ada", [1, 6 * D])
    gpre1_d = din("gpre1", [128, 8])
    gpre2_d = din("gpre2", [128, 8])
    gpost1_d = din("gpost1", [1, D])
    gpost2_d = din("gpost2", [1, D])
    win_d = din("w_in", [D, 1440])
    gq_d = din("gq", [128, 2])
    wuq_d = din("w_uq", [256, 768])
    gkv_d = din("gkv", [128, 1])
    wukv_d = din("w_ukv", [128, 1024])
    lng_d = din("lng", [1, 512])
    lnb_d = din("lnb", [1, 512])
    wsT_d = din("wsT", [128, 1024])
    bs_d = din("bs", [128, 8])
    wout_d = din("w_out", [D, D])
    wup_d = din("w_up", [D, 2 * DFF])
    convw_d = din("convw", [128, 132])
    convb_d = din("convb", [128, 44])
    wdn_d = din("w_down", [DFF, D])
    cos_d = din("cosT", [128, S])
    sin_d = din("sinT", [128, S])
    out_d = P.dram("out", [S, D], F32, "ExternalOutput")

    SBCOLS = 106000
    SB = P.sbuf_root("SB", SBCOLS)
    off = [0]

    def carve(F, dt, at=None):
        if at is None:
            at = off[0]
            off[0] = at + ((F * DTSZ[dt] + 3) // 4) * 4
        return Sub(SB, at, F, dt)

    KT = carve(8 * 4096, BF16)
    VA = carve(32 * 520 + 64, BF16)
    WIN = carve(8 * 1472, BF16)
    WUQ = carve(2 * 1024, BF16)
    WUKV = carve(1024, BF16)
    WOUT = carve(8 * 1024, BF16)
    WST = carve(8 * 128, BF16)
    PH2_END = 45056 + 90112
    assert off[0] >= PH2_END
    IDENT = carve(128, BF16)
    NEGH = carve(512, F32)
    GG1 = carve(1024, F32)
    GG2 = carve(1024, F32)
    COLS = carve(64, F32)
    GBC = carve(512, F32)
    CST = carve(512, F32)
    BSC = carve(8, F32)
    CONVW = carve(132, F32)
    CONVB = carve(44, F32)
    ST = carve(128, F32)
    ONESF = carve(128, F32)
    IDENTF = carve(128, F32)
    CARRY = [carve(44 * 2, F32) for _ in range(2)]
    R0 = off[0]
    RSZ = SBCOLS * 2 - R0

    def rcarve(base, F, dt):
        b = R0 + base
        assert base + F * DTSZ[dt] <= RSZ, (base, F, RSZ)
        return Sub(SB, b, F, dt)

    WDN = Sub(SB, 0, 22 * 1024, BF16)
    WUP = Sub(SB, 45056, 8 * 5632, BF16)

    RT_, RM_, RS_, RO_ = (P.ps(n, 1024) for n in ("psT", "psM", "psS", "psO"))
    PT = [PB(RT_, i, BF16) for i in range(2)]
    PM = [PB(RM_, i) for i in range(2)]
    PSS = [PB(RS_, i) for i in range(2)]
    PO = [PB(RO_, i) for i in range(2)]

    def rd(*vs):
        return [v for v in vs if isinstance(v, View)]

    def A(x):
        return x.ap if isinstance(x, View) else x

    def mm(out, lhsT, rhs, start, stop):
        P.op("pe", lambda e: e.matmul(out.ap, lhsT=lhsT.ap, rhs=rhs.ap, start=start, stop=stop), [lhsT, rhs], [out])

    def tr(out, in_):
        idv = IDENT.v(0, in_.ap.shape[0], 0, in_.ap.shape[0])
        P.op("pe", lambda e: e.transpose(out=out.ap, in_=in_.ap, identity=idv.ap), [in_, idv], [out])

    def act(out, in_, func, scale=1.0, bias=None, accum=None):
        kw = {}
        if bias is not None:
            kw["bias"] = A(bias)
        if accum is not None:
            kw["accum_out"] = accum.ap
        P.op("act", lambda e: e.activation(out=out.ap, in_=in_.ap, func=func, scale=A(scale), **kw),
             [in_] + rd(scale, bias), [out] + rd(accum))

    def ts(eng, out, in0, s1, s2, op0, op1=None):
        kw = {} if op1 is None else {"op1": op1}
        P.op(eng, lambda e: e.tensor_scalar(out=out.ap, in0=in0.ap, scalar1=A(s1), scalar2=A(s2), op0=op0, **kw),
             [in0] + rd(s1, s2), [out])

    def tt(eng, out, in0, in1, op):
        P.op(eng, lambda e: e.tensor_tensor(out=out.ap, in0=in0.ap, in1=in1.ap, op=op), [in0, in1], [out])

    def stt(out, in0, scalar, in1, op0, op1):
        P.op("dve", lambda e: e.scalar_tensor_tensor(out=out.ap, in0=in0.ap, scalar=A(scalar), in1=in1.ap, op0=op0, op1=op1),
             [in0, in1] + rd(scalar), [out])

    def cp(eng, out, in_):
        if eng == "act":
            P.op("act", lambda e: e.activation(out=out.ap, in_=in_.ap, func=AF.Copy), [in_], [out])
        else:
            P.op(eng, lambda e: e.tensor_copy(out=out.ap, in_=in_.ap), [in_], [out])

    def memset(eng, out, val):
        P.op(eng, lambda e: e.memset(out.ap, val), [], [out])

    def red(out, in_, op=ALU.add):
        P.op("dve", lambda e: e.tensor_reduce(out=out.ap, in_=in_.ap, axis=AX.X, op=op), [in_], [out])

    def recip(out, in_):
        P.op("dve", lambda e: e.reciprocal(out=out.ap, in_=in_.ap), [in_], [out])

    def bview(v, pat_shape):
        return View(v.ap.to_broadcast(pat_shape), v.reg)

    def rstd_from(out, ss, n, width):
        ts("dve", out, ss, 1.0 / n, EPS, ALU.mult, ALU.add)
        p0, p1 = out.reg[1], out.reg[2]
        tt("pool", out, out, NEGH.v(p0, p1, 0, width), ALU.pow)

    memset("pool", NEGH.v(), -0.5)
    memset("pool", ONESF.v(), 1.0)
    identf = IDENTF
    memset("pool", identf.v(), 1.0)
    P.op("pool", lambda e: e.affine_select(out=identf.v().ap, in_=identf.v().ap, pattern=[[-1, 128]],
                                            compare_op=ALU.is_equal, fill=0.0, base=0, channel_multiplier=1),
         [identf.v()], [identf.v()])
    cp("dve", IDENT.v(), identf.v())
    memset("pool", CARRY[0].v(), 0.0)

    ccol = rcarve(512, 8, F32)
    P.dma(ccol.v(), ccol_d.v(), "c0")
    P.dma(COLS.v(0, 128, 32, 34), gq_d.v(), "c0")
    P.dma(COLS.v(0, 128, 34, 35), gkv_d.v(), "c0")
    P.dma(BSC.v(), bs_d.v(), "c0")
    P.dma(CONVW.v(), convw_d.v(), "c0")
    P.dma(CONVB.v(), convb_d.v(), "c0")
    gpre = rcarve(544, 16, F32)
    P.dma(gpre.v(0, 128, 0, 8), gpre1_d.v(), "c0")
    P.dma(gpre.v(0, 128, 8, 16), gpre2_d.v(), "c0")

    cact = rcarve(608, 8, F32)
    act(cact.v(), ccol.v(), AF.Silu)
    crep = rcarve(1024, 8 * 128, F32)
    for kc in range(8):
        ts("dve", crep.v(0, 128, kc * 128, (kc + 1) * 128), identf.v(), 0.0, cact.v(0, 128, kc, kc + 1), ALU.mult, ALU.add)
    ADA = KT.alias(F32)
    BADA = Sub(SB, KT.off + 6144 * 4, 6144, F32)
    P.dma(BADA.v(), View(bada_d.h[0:1, :].partition_broadcast(128), bada_d.v().reg), "c0")
    wst = [rcarve(8192 + i * 16384, 8 * 512, F32) for i in range(2)] + [Sub(SB, VA.off + i * 16384, 8 * 512, F32) for i in range(2)]
    for n in range(12):
        st_ = wst[n % 4]
        P.dma(st_.v().re("p (c n) -> p c n", c=8),
              View(wada_d.h[:, n * 512:(n + 1) * 512].rearrange("(c p) n -> p c n", p=128), wada_d.v(0, D, n * 512, (n + 1) * 512).reg),
              f"wa{n % 4}")
        pm = PM[n % 2]
        for kc in range(8):
            mm(pm.v(), crep.v(0, 128, kc * 128, (kc + 1) * 128), st_.v(0, 128, kc * 512, (kc + 1) * 512), kc == 0, kc == 7)
        tt("dve", ADA.v(0, 128, n * 512, (n + 1) * 512), pm.v(), BADA.v(0, 128, n * 512, (n + 1) * 512), ALU.add)

    memset("pool", VA.v(0, 128, 32 * 520, 32 * 520 + 64), 0.0)
    P.op("pool", lambda e: e.memset(VA.v(0, 128, 0, 32 * 520).ap.rearrange("p (k c) -> p k c", c=65)[:, :, 64:65], 1.0), [], [VA.v(0, 128, 0, 32 * 520)])
    tmpx = rcarve(8192, 1024, F32)
    colx = rcarve(640, 32, F32)
    for i, base in enumerate((0, 1024, 3072, 4096)):
        P.op("dve", lambda e, base=base: e.tensor_tensor(
            out=tmpx.v().ap.rearrange("p (j n) -> p j n", j=8),
            in0=ADA.v(0, 128, base, base + 1024).ap.rearrange("p (j n) -> p j n", j=8),
            in1=identf.v().ap.unsqueeze(1).to_broadcast([128, 8, 128]), op=ALU.mult),
            [ADA.v(0, 128, base, base + 1024), identf.v()], [tmpx.v()])
        red(colx.v(0, 128, i * 8, (i + 1) * 8), tmpx.v().re("p (j n) -> p j n", j=8))
    ts("dve", colx.v(0, 128, 8, 16), colx.v(0, 128, 8, 16), 1.0, None, ALU.add)
    ts("dve", colx.v(0, 128, 24, 32), colx.v(0, 128, 24, 32), 1.0, None, ALU.add)
    tt("dve", COLS.v(0, 128, 0, 8), colx.v(0, 128, 8, 16), gpre.v(0, 128, 0, 8), ALU.mult)
    tt("dve", COLS.v(0, 128, 16, 24), colx.v(0, 128, 24, 32), gpre.v(0, 128, 8, 16), ALU.mult)
    cp("dve", COLS.v(0, 128, 8, 16), colx.v(0, 128, 0, 8))
    cp("dve", COLS.v(0, 128, 24, 32), colx.v(0, 128, 16, 24))
    P.dma(GG1.v(), View(gpost1_d.h[0:1, :].partition_broadcast(128), gpost1_d.v().reg), "c0")
    P.dma(GG2.v(), View(gpost2_d.h[0:1, :].partition_broadcast(128), gpost2_d.v().reg), "c0")
    tt("dve", GG1.v(), GG1.v(), ADA.v(0, 128, 2048, 3072), ALU.mult)
    tt("dve", GG2.v(), GG2.v(), ADA.v(0, 128, 5120, 6144), ALU.mult)

    P.dma(GBC.v(), View(lng_d.h[0:1, :].partition_broadcast(128), lng_d.v().reg), "c0")
    P.dma(CST.v(), View(lnb_d.h[0:1, :].partition_broadcast(128), lnb_d.v().reg), "c0")

    wstage = [rcarve(8192 + i * 16384, 4096, F32) for i in range(2)]
    cast_engs = ["dve", "pool", "act"]
    cnt = [0]

    def stage_load(src_view_ap, src_reg, ncols):
        st_ = wstage[cnt[0] % 2]
        tagn = f"ws{cnt[0] % 2}"
        cnt[0] += 1
        P.dma(st_.v(0, 128, 0, ncols), View(src_view_ap, src_reg), tagn)
        return st_

    def cast(dst, src):
        eng = cast_engs[cnt[0] % 3]
        cp(eng, dst, src)

    for kc in range(8):
        st_ = stage_load(win_d.h[kc * 128:(kc + 1) * 128, :], win_d.v(kc * 128, (kc + 1) * 128).reg, 1440)
        b = kc * 1472
        cp("dve", WIN.v(0, 128, b, b + 416), st_.v(0, 128, 0, 416))
        cp("pool", WIN.v(0, 128, b + 416, b + 432), st_.v(0, 128, 400, 416))
        cp("pool", WIN.v(0, 128, b + 432, b + 448), st_.v(0, 128, 384, 400))
        cp("act", WIN.v(0, 128, b + 448, b + 1472), st_.v(0, 128, 416, 1440))
    for kc in range(2):
        st_ = stage_load(wuq_d.h[kc * 128:(kc + 1) * 128, :], wuq_d.v(kc * 128, (kc + 1) * 128).reg, 768)
        b = kc * 1024
        gq = COLS.v(0, 128, 32 + kc, 33 + kc)
        s3 = st_.v(0, 128, 0, 768).re("p (h c) -> p h c", h=8)
        P.op("dve", lambda e, b=b, s3=s3, gq=gq: e.tensor_scalar(
            out=WUQ.v(0, 128, b, b + 512).ap.rearrange("p (h c) -> p h c", h=8), in0=s3.ap[:, :, 0:64],
            scalar1=gq.ap, scalar2=None, op0=ALU.mult), [s3, gq], [WUQ.v(0, 128, b, b + 512)])
        P.op("dve", lambda e, b=b, s3=s3, gq=gq: e.tensor_scalar(
            out=WUQ.v(0, 128, b + 512, b + 768).ap.rearrange("p (h c) -> p h c", h=8), in0=s3.ap[:, :, 64:96],
            scalar1=gq.ap, scalar2=None, op0=ALU.mult), [s3, gq], [WUQ.v(0, 128, b + 512, b + 768)])
        P.op("dve", lambda e, b=b, s3=s3, gq=gq: e.tensor_scalar(
            out=WUQ.v(0, 128, b + 768, b + 1024).ap.rearrange("p (h c) -> p h c", h=8)[:, :, 0:16], in0=s3.ap[:, :, 80:96],
            scalar1=gq.ap, scalar2=None, op0=ALU.mult), [s3, gq], [WUQ.v(0, 128, b + 768, b + 1024)])
        P.op("dve", lambda e, b=b, s3=s3, gq=gq: e.tensor_scalar(
            out=WUQ.v(0, 128, b + 768, b + 1024).ap.rearrange("p (h c) -> p h c", h=8)[:, :, 16:32], in0=s3.ap[:, :, 64:80],
            scalar1=gq.ap, scalar2=None, op0=ALU.mult), [s3, gq], [WUQ.v(0, 128, b + 768, b + 1024)])
    st_ = stage_load(wukv_d.h[:, :], wukv_d.v().reg, 1024)
    gkv = COLS.v(0, 128, 34, 35)
    s3 = st_.v(0, 128, 0, 1024).re("p (h c) -> p h c", h=8)
    P.op("dve", lambda e, s3=s3: e.tensor_scalar(
        out=WUKV.v(0, 128, 0, 512).ap.rearrange("p (h c) -> p h c", h=8), in0=s3.ap[:, :, 0:64],
        scalar1=gkv.ap, scalar2=None, op0=ALU.mult), [s3, gkv], [WUKV.v(0, 128, 0, 512)])
    P.op("dve", lambda e, s3=s3: e.tensor_scalar(
        out=WUKV.v(0, 128, 512, 1024).ap.rearrange("p (h c) -> p h c", h=8), in0=s3.ap[:, :, 64:128],
        scalar1=gkv.ap, scalar2=None, op0=ALU.mult), [s3, gkv], [WUKV.v(0, 128, 512, 1024)])
    for kc in range(8):
        st_ = stage_load(wout_d.h[kc * 128:(kc + 1) * 128, :], wout_d.v(kc * 128, (kc + 1) * 128).reg, 1024)
        cast(WOUT.v(0, 128, kc * 1024, (kc + 1) * 1024), st_.v(0, 128, 0, 1024))
    st_ = stage_load(wsT_d.h[:, :], wsT_d.v().reg, 1024)
    P.op("pool", lambda e, st_=st_: e.memset(st_.v(64, 128, 0, 1024).ap.rearrange("p (h i) -> p h i", h=8)[:, :, 0:64], 0.0),
         [], [st_.v(64, 128, 0, 1024)])
    cp("dve", WST.v(), st_.v(0, 128, 0, 1024))
    onesb = rcarve(768, 8, BF16)
    memset("pool", onesb.v(), 1.0)
    rs = rcarve(800, 8, F32)
    for h in range(8):
        mm(PM[0].v(0, 128, h, h + 1), WST.v(0, 128, h * 128, (h + 1) * 128), onesb.v(0, 128, 0, 1), True, True)
    cp("dve", rs.v(), PM[0].v(0, 128, 0, 8))
    for h in range(8):
        ts("dve", CST.v(0, 128, h * 64, (h + 1) * 64), CST.v(0, 128, h * 64, (h + 1) * 64),
           rs.v(0, 128, h, h + 1), BSC.v(0, 128, h, h + 1), ALU.mult, ALU.add)

    HT = rcarve(0, 8 * 512, BF16)
    QT = rcarve(8192, 8 * 512, BF16)
    SGUT = rcarve(16384, 4 * 512, BF16)
    AT = rcarve(20480, 4 * 512, BF16)
    U0 = 24576
    XIN = [rcarve(U0 + 16384, 1024, F32), rcarve(U0 + 12288, 1024, F32)]
    XN = rcarve(0, 4096, BF16)
    CQF = rcarve(U0, 3 * 512, F32)
    SQ = rcarve(U0 + 6144, 2048, F32)
    DIAG = rcarve(U0 + 14336, 1024, F32)
    CQN = rcarve(U0 + 18432, 3 * 512, BF16)
    TMP1 = rcarve(U0, 512, F32)
    TMP2 = rcarve(U0 + 2048, 512, F32)
    ROT = rcarve(U0 + 4096, 512, BF16)
    CS = rcarve(U0 + 6144, 1024, F32)
    GBs = [rcarve(U0 + i * 10240, 1024, F32) for i in range(2)]
    SQ2s = [rcarve(U0 + i * 10240 + 4096, 512, F32) for i in range(2)]
    VNs = [rcarve(U0 + i * 10240 + 6144, 512, BF16) for i in range(2)]
    MIXs = [rcarve(U0 + i * 10240 + 7168, 512, F32) for i in range(2)]
    SGUs = [rcarve(U0 + i * 10240 + 9216, 512, BF16) for i in range(2)]
    PTBP = [rcarve(U0, 1024, BF16), rcarve(U0 + 2048, 1024, BF16), rcarve(U0 + 20480, 1024, BF16)]
    RDEN = [rcarve(U0 + 4096 + i * 2048, 512, F32) for i in range(2)]
    RDEN2 = [rcarve(U0 + 8192 + i * 2048, 512, F32) for i in range(2)]
    XRES = [rcarve(U0 + i * 4096, 1024, F32) for i in range(2)]
    TT_ = [rcarve(U0 + 8192 + i * 4096, 1024, F32) for i in range(2)]

    def tbank(fc, banks=None):
        banks = banks or [PT[0], PT[1], PM[0].as_bf16(), PM[1].as_bf16()]
        pb = banks[fc // 2]
        return lambda c0, c1: pb.v(0, 128, c0, c1)

    def prep_a(src_d, t0, XINb, XNb):
        for sbk in range(4):
            r0 = t0 + sbk * 128
            xin = XINb[sbk % 2]
            xn = XNb.v(0, 128, sbk * 1024, (sbk + 1) * 1024)
            P.dma(xin.v(), src_d.v(r0, r0 + 128), f"xin{sbk % 2}")
            act(xn, xin.v(), AF.Square, accum=ST.v(0, 128, 2 * sbk, 2 * sbk + 1))
            rstd_from(ST.v(0, 128, 2 * sbk + 1, 2 * sbk + 2), ST.v(0, 128, 2 * sbk, 2 * sbk + 1), D, 1)
            ts("dve", xn, xin.v(), ST.v(0, 128, 2 * sbk + 1, 2 * sbk + 2), None, ALU.mult)

    def prep_b(gmod_c, sh_c, HTbuf, XNb, banks=None):
        for fc in range(8):
            bank = tbank(fc, banks)
            for sbk in range(4):
                c0 = (fc % 2) * 512 + sbk * 128
                tr(bank(c0, c0 + 128), XNb.v(0, 128, sbk * 1024 + fc * 128, sbk * 1024 + (fc + 1) * 128))
        for fc in range(8):
            bank = tbank(fc, banks)
            c0 = (fc % 2) * 512
            act(HTbuf.v(0, 128, fc * 512, (fc + 1) * 512), bank(c0, c0 + 512), AF.Identity,
                scale=COLS.v(0, 128, gmod_c + fc, gmod_c + fc + 1), bias=COLS.v(0, 128, sh_c + fc, sh_c + fc + 1))

    def prep_hT(src_d, t0, gmod_c, sh_c, HTbuf, XINb, XNb):
        prep_a(src_d, t0, XINb, XNb)
        prep_b(gmod_c, sh_c, HTbuf, XNb)

    def post_norm_residual(pm_pair, src_d, dst_d, r0, GG, slot, XRESb, TTb):
        xr = XRESb[slot]
        tb = TTb[slot]
        tbj = tb.alias(BF16)
        s0 = 8 + 4 * slot
        P.dma(xr.v(), src_d.v(r0, r0 + 128), f"xr{slot}")
        for hf in range(2):
            act(tbj.v(0, 128, hf * 512, (hf + 1) * 512), pm_pair[hf].v(), AF.Square, accum=ST.v(0, 128, s0 + hf, s0 + hf + 1))
        tt("dve", ST.v(0, 128, s0 + 2, s0 + 3), ST.v(0, 128, s0, s0 + 1), ST.v(0, 128, s0 + 1, s0 + 2), ALU.add)
        rstd_from(ST.v(0, 128, s0 + 3, s0 + 4), ST.v(0, 128, s0 + 2, s0 + 3), D, 1)
        for hf in range(2):
            tt("dve", tb.v(0, 128, hf * 512, (hf + 1) * 512), pm_pair[hf].v(), GG.v(0, 128, hf * 512, (hf + 1) * 512), ALU.mult)
        stt(xr.v(), tb.v(), ST.v(0, 128, s0 + 3, s0 + 4), xr.v(), ALU.mult, ALU.add)
        P.dma(dst_d.v(r0, r0 + 128), xr.v(), f"xo{slot}")

    for T in range(NT):
        t0 = T * 512
        if T == 0:
            prep_a(x_d, t0, XIN, XN)
        if T == 0:
            prep_b(0, 8, HT, XN)
        for c in range(3):
            pm = PM[c % 2]
            for kc in range(8):
                mm(pm.v(), WIN.v(0, 128, kc * 1472 + c * 128, kc * 1472 + (c + 1) * 128), HT.v(0, 128, kc * 512, (kc + 1) * 512), kc == 0, kc == 7)
            cp("act", CQF.v(0, 128, c * 512, (c + 1) * 512), pm.v())
        def norm_steps(grp, chunks, nfeat):
            pss = PSS[grp]
            pbc = PM[grp]
            sc0 = 40 + grp * 4

            def squares():
                for i, c in enumerate(chunks):
                    act(SQ.v(0, 128, (grp * 2 + i) * 512, (grp * 2 + i + 1) * 512), CQF.v(0, 128, c * 512, (c + 1) * 512), AF.Square)

            def sums():
                for sbk in range(4):
                    for i, c in enumerate(chunks):
                        b = (grp * 2 + i) * 512 + sbk * 128
                        mm(pss.v(0, 128, sbk, sbk + 1), SQ.v(0, 128, b, b + 128), ONESF.v(0, 128, 0, 1), i == 0, i == len(chunks) - 1)

            def bcast():
                for sbk in range(4):
                    ts("dve", DIAG.v(0, 128, grp * 512 + sbk * 128, grp * 512 + (sbk + 1) * 128), IDENTF.v(), ST.v(0, 128, sc0 + sbk, sc0 + sbk + 1), None, ALU.mult)
                    mm(pbc.v(0, 128, sbk * 128, (sbk + 1) * 128), ONESF.v(), DIAG.v(0, 128, grp * 512 + sbk * 128, grp * 512 + (sbk + 1) * 128), True, True)

            def apply():
                for c in chunks:
                    tt("dve", CQN.v(0, 128, c * 512, (c + 1) * 512), CQF.v(0, 128, c * 512, (c + 1) * 512), pbc.v(), ALU.mult)

            return [
                squares,
                sums,
                lambda: ts("dve", ST.v(0, 128, sc0, sc0 + 4), pss.v(0, 128, 0, 4), 1.0 / nfeat, EPS, ALU.mult, ALU.add),
                lambda: tt("pool", ST.v(0, 128, sc0, sc0 + 4), ST.v(0, 128, sc0, sc0 + 4), NEGH.v(0, 128, 0, 4), ALU.pow),
                bcast,
                apply,
            ]

        for fa, fb in zip(norm_steps(0, (0, 1), 256), norm_steps(1, (2,), 128)):
            fa()
            fb()
        P.dma(CS.v(0, 128, 0, 512), cos_d.v(0, 128, t0, t0 + 512), "cs")
        P.dma(CS.v(0, 128, 512, 1024), sin_d.v(0, 128, t0, t0 + 512), "cs")
        for sw in range(2):
            pm = PM[sw]
            for kc in range(8):
                b = kc * 1472 + 384 + sw * 32
                mm(pm.v(64, 96), WIN.v(0, 128, b, b + 32), HT.v(0, 128, kc * 512, (kc + 1) * 512), kc == 0, kc == 7)
        tt("dve", TMP1.v(64, 96), PM[0].v(64, 96), CS.v(64, 96, 0, 512), ALU.mult)
        tt("dve", TMP2.v(64, 96), PM[1].v(64, 96), CS.v(64, 96, 512, 1024), ALU.mult)
        tt("pool", ROT.v(64, 96), TMP1.v(64, 96), TMP2.v(64, 96), ALU.add)
        for h in range(8):
            cp("dve", KT.v(64, 96, h * 4096 + t0, h * 4096 + t0 + 512), ROT.v(64, 96))
        for g in range(2):
            pr, psw = PSS[0], PSS[1]
            for kc in range(2):
                b = kc * 1024 + 512 + g * 128
                mm(pr.v(), WUQ.v(0, 128, b, b + 128), CQN.v(0, 128, kc * 512, (kc + 1) * 512), kc == 0, kc == 1)
            for kc in range(2):
                b = kc * 1024 + 768 + g * 128
                mm(psw.v(), WUQ.v(0, 128, b, b + 128), CQN.v(0, 128, kc * 512, (kc + 1) * 512), kc == 0, kc == 1)
            tt("dve", TMP1.v(), pr.v(), CS.v(0, 128, 0, 512), ALU.mult)
            tt("dve", TMP2.v(), psw.v(), CS.v(0, 128, 512, 1024), ALU.mult)
            tt("pool", ROT.v(), TMP1.v(), TMP2.v(), ALU.add)
            for hh in range(4):
                h = g * 4 + hh
                cp(("dve", "act")[hh % 2], QT.v(64, 96, h * 512, (h + 1) * 512), ROT.v(32 * hh, 32 * hh + 32))
            for hp in range(2 * g, 2 * g + 2):
                pa = PM[hp % 2]
                for kc in range(2):
                    b = kc * 1024 + hp * 128
                    mm(pa.v(), WUQ.v(0, 128, b, b + 128), CQN.v(0, 128, kc * 512, (kc + 1) * 512), kc == 0, kc == 1)
                cp("act", QT.v(0, 64, (2 * hp) * 512, (2 * hp + 1) * 512), pa.v(0, 64))
                cp("dve", QT.v(0, 64, (2 * hp + 1) * 512, (2 * hp + 2) * 512), pa.v(64, 128))
        for h in range(8):
            pk = PO[h % 2]
            mm(pk.v(0, 64), WUKV.v(0, 128, h * 64, (h + 1) * 64), CQN.v(0, 128, 1024, 1536), True, True)
            cp(("act", "dve")[h % 2], KT.v(0, 64, h * 4096 + t0, h * 4096 + t0 + 512), pk.v(0, 64))
        for sbk in range(4):
            kt = T * 4 + sbk
            pv = PO[sbk % 2]
            mm(pv.v(), CQN.v(0, 128, 1024 + sbk * 128, 1024 + (sbk + 1) * 128), WUKV.v(0, 128, 512, 1024), True, True)
            P.op("dve", lambda e, kt=kt, pv=pv: e.tensor_copy(
                out=VA.v(0, 128, kt * 520, (kt + 1) * 520).ap.rearrange("p (h c) -> p h c", c=65)[:, :, 0:64],
                in_=pv.v().ap.rearrange("p (h c) -> p h c", c=64)), [pv.v()], [VA.v(0, 128, kt * 520, (kt + 1) * 520)])
        def gm_Gmm(sbk):
            i2 = sbk % 2
            gbank = PM if i2 == 0 else PO
            for hf in range(2):
                pm = gbank[hf]
                for kc in range(8):
                    b = kc * 1472 + 448 + hf * 512
                    mm(pm.v(), HT.v(0, 128, kc * 512 + sbk * 128, kc * 512 + (sbk + 1) * 128), WIN.v(0, 128, b, b + 512), kc == 0, kc == 7)

        def gm_Gact(sbk):
            i2 = sbk % 2
            gbank = PM if i2 == 0 else PO
            for hf in range(2):
                act(GBs[i2].v(0, 128, hf * 512, (hf + 1) * 512), gbank[hf].v(), AF.Gelu_apprx_tanh)

        def gm_S(sbk):
            i2 = sbk % 2
            GB, SQ2, VN, MIX, SGU = GBs[i2], SQ2s[i2], VNs[i2], MIXs[i2], SGUs[i2]
            sb0 = 16 if i2 == 0 else 64
            sA, sB, sC = sb0, sb0 + 8, sb0 + 16
            vv = GB.v(0, 128, 512, 1024)
            uu = GB.v(0, 128, 0, 512)
            h3 = lambda v: v.re("p (h d) -> p h d", h=8)
            mean_b = View(ST.v(0, 128, sA, sA + 8).ap.unsqueeze(2).to_broadcast([128, 8, 64]), ST.v(0, 128, sA, sA + 8).reg)
            rstd_b = View(ST.v(0, 128, sB, sB + 8).ap.unsqueeze(2).to_broadcast([128, 8, 64]), ST.v(0, 128, sB, sB + 8).reg)
            pm = PSS[sbk % 2]
            pt = PT[sbk % 2]

            def spatial():
                for h in range(8):
                    mm(pm.v(0, 128, h * 64, (h + 1) * 64), WST.v(0, 128, h * 128, (h + 1) * 128), VN.v(0, 128, h * 64, (h + 1) * 64), True, True)

            def transposes():
                for c in range(4):
                    tr(pt.v(0, 128, c * 128, (c + 1) * 128), SGU.v(0, 128, c * 128, (c + 1) * 128))

            return [
                lambda: red(ST.v(0, 128, sA, sA + 8), h3(vv)),
                lambda: act(SQ2.v(), vv, AF.Square),
                lambda: red(ST.v(0, 128, sB, sB + 8), h3(SQ2.v())),
                lambda: ts("dve", ST.v(0, 128, sA, sA + 8), ST.v(0, 128, sA, sA + 8), 1.0 / 64, None, ALU.mult),
                lambda: tt("dve", ST.v(0, 128, sC, sC + 8), ST.v(0, 128, sA, sA + 8), ST.v(0, 128, sA, sA + 8), ALU.mult),
                lambda: stt(ST.v(0, 128, sB, sB + 8), ST.v(0, 128, sB, sB + 8), 1.0 / 64, ST.v(0, 128, sC, sC + 8), ALU.mult, ALU.subtract),
                lambda: ts("dve", ST.v(0, 128, sB, sB + 8), ST.v(0, 128, sB, sB + 8), EPS, None, ALU.add),
                lambda: tt("pool", ST.v(0, 128, sB, sB + 8), ST.v(0, 128, sB, sB + 8), NEGH.v(0, 128, 0, 8), ALU.pow),
                lambda: tt("dve", h3(SQ2.v()), h3(vv), mean_b, ALU.subtract),
                lambda: tt("dve", h3(VN.v()), h3(SQ2.v()), rstd_b, ALU.mult),
                spatial,
                lambda: tt("dve", MIX.v(), pm.v(), GBC.v(), ALU.mult),
                lambda: tt("dve", MIX.v(), MIX.v(), CST.v(), ALU.add),
                lambda: tt("pool", SGU.v(), MIX.v(), uu, ALU.mult),
                transposes,
                lambda: P.op("act", lambda e: e.activation(
                    out=SGUT.v().ap.rearrange("p (c t) -> p c t", c=4)[:, :, sbk * 128:(sbk + 1) * 128],
                    in_=pt.v(0, 128, 0, 512).ap.rearrange("p (c t) -> p c t", c=4), func=AF.Copy),
                    [pt.v()], [SGUT.v()]),
            ]

        def lockstep(a, b):
            for fa, fb in zip(a, b):
                fa()
                fb()

        gm_Gmm(0)
        gm_Gact(0)
        gm_Gmm(1)
        gm_Gact(1)
        gm_Gmm(2)
        gm_Gmm(3)
        lockstep(gm_S(0), gm_S(1))
        gm_Gact(2)
        gm_Gact(3)
        lockstep(gm_S(2), gm_S(3))

        nkt = 4 * T + 4
        pairs = [(h, j) for h in range(8) for j in range(nkt // 2)]
        SROOT = [RS_, RT_, RM_]

        def geom(kt):
            r = kt - 4 * T
            q0 = 128 * r if r > 0 else 0
            return r, q0, 512 - q0

        def qk(g):
            h, j = pairs[g]
            for e in range(2):
                kt = 2 * j + e
                r, q0, n = geom(kt)
                mm(PB(SROOT[g % 3], e).v(0, 128, 0, n), KT.v(0, 96, h * 4096 + kt * 128, h * 4096 + (kt + 1) * 128),
                   QT.v(0, 96, h * 512 + q0, (h + 1) * 512), True, True)

        def do_exp(g):
            h, j = pairs[g]
            sr = SROOT[g % 3]
            ptb = PTBP[g % 3]
            ge = [geom(2 * j + e) for e in range(2)]
            if ge[0][2] == 512 and ge[1][2] == 512:
                act(ptb.v(0, 128, 0, 1024), View(sr.h[:, 0:1024], (sr.id, 0, 128, 0, 1024)), AF.Exp, scale=ATTN_SCALE)
            else:
                for e in range(2):
                    n = ge[e][2]
                    act(ptb.v(0, 128, e * 512, e * 512 + n), PB(sr, e).v(0, 128, 0, n), AF.Exp, scale=ATTN_SCALE)
            for e in range(2):
                if ge[e][0] >= 0:
                    memset("pool", ptb.v(64, 128, e * 512, e * 512 + 64), 0.0)

        def do_pv(g):
            h, j = pairs[g]
            ptb = PTBP[g % 3]
            po = PO[h % 2]
            for e in range(2):
                kt = 2 * j + e
                r, q0, n = geom(kt)
                voff = kt * 520 + h * 65
                mm(po.v(0, 128, q0, 512), VA.v(0, 128, voff, voff + 128), ptb.v(0, 128, e * 512, e * 512 + n), kt == 0, kt == nkt - 1)
            if h == 1 and j == nkt // 2 - 1 and T + 1 < NT:
                prep_a(x_d, t0 + 512, XIN, XN)
            if j == nkt // 2 - 1:
                rd_ = RDEN[h % 2]
                c, half = h // 2, h % 2
                p0 = half * 64
                recip(rd_.v(64, 65), po.v(64, 65))
                mm(po.v(64, 128), ONESF.v(64, 65, 0, 64), rd_.v(64, 65), True, True)
                bc = RDEN2[h % 2].v(64, 128)
                cp("act", bc, po.v(64, 128))
                tt("dve", AT.v(p0, p0 + 64, c * 512, (c + 1) * 512), po.v(0, 64), bc, ALU.mult)

        qk(0)
        qk(1)
        for g in range(len(pairs) + 1):
            if g + 2 < len(pairs):
                qk(g + 2)
            if g < len(pairs):
                do_exp(g)
            if g >= 1:
                do_pv(g - 1)

        for sbk in range(4):
            pair = (PM[0], PM[1]) if sbk % 2 == 0 else (PSS[0], PSS[1])
            for hf in range(2):
                for kc in range(8):
                    src = AT if kc < 4 else SGUT
                    cc = kc % 4
                    mm(pair[hf].v(), src.v(0, 128, cc * 512 + sbk * 128, cc * 512 + (sbk + 1) * 128),
                       WOUT.v(0, 128, kc * 1024 + hf * 512, kc * 1024 + (hf + 1) * 512), kc == 0, kc == 7)
            post_norm_residual(pair, x_d, out_d, t0 + sbk * 128, GG1, sbk % 2, XRES, TT_)
            if sbk == 1 and T + 1 < NT:
                prep_b(0, 8, HT, XN, [PT[0], PT[1], PO[0].as_bf16(), PO[1].as_bf16()])

    if not phase2:
        P.finalize()
        P.emit()
        return nc

    H2T = rcarve(0, 8 * 512, BF16)
    ACTT = rcarve(8192, 22 * 512, BF16)
    XRES2 = [rcarve(30720 + i * 4096, 1024, F32) for i in range(2)]
    TT2 = [rcarve(38912 + i * 4096, 1024, F32) for i in range(2)]
    XIN2 = [Sub(SB, PH2_END + i * 4096, 1024, F32) for i in range(2)]
    XN2 = rcarve(0, 4096, BF16)
    YAB = [[Sub(SB, PH2_END + (ab * 2 + i) * 2048, 512, F32) for i in range(2)] for ab in range(2)]
    assert PH2_END + 8192 <= IDENT.off, (PH2_END, IDENT.off)
    st2 = [rcarve(30720 + i * 4096, 1024, F32) for i in range(3)]
    k2 = [0]

    def load_wdn(c):
        s_ = st2[k2[0] % 3]
        P.dma(s_.v(), wdn_d.v(c * 128, (c + 1) * 128), f"w2{k2[0] % 3}")
        cp(("dve", "act")[k2[0] % 2], WDN.v(0, 128, c * 1024, (c + 1) * 1024), s_.v())
        k2[0] += 1

    def load_wup(c, ab):
        s_ = st2[k2[0] % 3]
        col = ab * DFF + c * 128
        P.dma(s_.v().re("p (k n) -> p k n", k=8),
              View(wup_d.h[:, col:col + 128].rearrange("(k p) n -> p k n", p=128), wup_d.v(0, D, col, col + 128).reg),
              f"w2{k2[0] % 3}")
        base = WUP.off // 2 + col
        dst = View(SB.h[:, base:base + 8 * 5632].rearrange("p (k n) -> p k n", n=5632)[:, :, 0:128],
                   WUP.v(0, 128, col, col + 128).reg).also(*[WUP.v(0, 128, kc * 5632 + col, kc * 5632 + col + 128).reg for kc in range(1, 8)])
        eng = ("act", "pool")[k2[0] % 2]
        src = s_.v().re("p (k n) -> p k n", k=8)
        if eng == "act":
            P.op("act", lambda e: e.activation(out=dst.ap, in_=src.ap, func=AF.Copy), [src], [dst])
        else:
            P.op("pool", lambda e: e.tensor_copy(out=dst.ap, in_=src.ap), [src], [dst])
        k2[0] += 1

    prep_a(out_d, 0, XIN2, XN2)
    prep_b(16, 24, H2T, XN2)
    for c in range(NCH):
        load_wdn(c)
    LA = 2
    RSM0, RSM1 = PM[0].as_bf16(), PM[1].as_bf16()

    for T in range(NT):
        t0 = T * 512
        cold = CARRY[T % 2]
        cnew = CARRY[(T + 1) % 2]
        if T == 0:
            for c in range(LA):
                load_wup(c, 0)
                load_wup(c, 1)
        for c in range(NCH):
            if T == 0 and c + LA < NCH:
                load_wup(c + LA, 0)
                load_wup(c + LA, 1)
            ys = []
            for ab in range(2):
                pm = (PM, PSS, PO)[c % 3][ab]
                col = ab * DFF + c * 128
                for kc in range(8):
                    mm(pm.v(), WUP.v(0, 128, kc * 5632 + col, kc * 5632 + col + 128), H2T.v(0, 128, kc * 512, (kc + 1) * 512), kc == 0, kc == 7)
                ci = ab * NCH + c
                y = YAB[ab][c % 2]
                ys.append(y)
                w = lambda kk, ci=ci: CONVW.v(0, 128, ci * 3 + kk, ci * 3 + kk + 1)
                act(y.v(), pm.v(), AF.Identity, scale=w(2), bias=CONVB.v(0, 128, ci, ci + 1))
                stt(y.v(0, 128, 1, 512), pm.v(0, 128, 0, 511), w(1), y.v(0, 128, 1, 512), ALU.mult, ALU.add)
                stt(y.v(0, 128, 2, 512), pm.v(0, 128, 0, 510), w(0), y.v(0, 128, 2, 512), ALU.mult, ALU.add)
                stt(y.v(0, 128, 0, 2), cold.v(0, 128, ci * 2, ci * 2 + 2), w(0), y.v(0, 128, 0, 2), ALU.mult, ALU.add)
                stt(y.v(0, 128, 0, 1), cold.v(0, 128, ci * 2 + 1, ci * 2 + 2), w(1), y.v(0, 128, 0, 1), ALU.mult, ALU.add)
                cp("act", cnew.v(0, 128, ci * 2, ci * 2 + 2), pm.v(0, 128, 510, 512))
            act(ys[0].v(), ys[0].v(), AF.Silu)
            tt("pool", ACTT.v(0, 128, c * 512, (c + 1) * 512), ys[0].v(), ys[1].v(), ALU.mult)
        if T + 1 < NT:
            prep_a(out_d, t0 + 512, XIN2, XN2)
        for sbk in range(4):
            pair = (PO[0], PO[1]) if sbk % 2 == 0 else (PSS[0], PSS[1])
            for hf in range(2):
                for c in range(NCH):
                    mm(pair[hf].v(), ACTT.v(0, 128, c * 512 + sbk * 128, c * 512 + (sbk + 1) * 128),
                       WDN.v(0, 128, c * 1024 + hf * 512, c * 1024 + (hf + 1) * 512), c == 0, c == NCH - 1)
            post_norm_residual(pair, out_d, out_d, t0 + sbk * 128, GG2, sbk % 2, XRES2, TT2)
            if sbk == 1 and T + 1 < NT:
                prep_b(16, 24, H2T, XN2, [PT[0], PT[1], RSM0, RSM1])

    P.finalize()
    P.emit()
    return nc


def _rope_tables():
    pos = np.arange(S, dtype=np.float32)
    inv = (np.float32(10000.0) ** (-np.arange(0, 32, 2, dtype=np.float32) / np.float32(32))).astype(np.float32)
    ang = pos[None, :] * inv[:, None]
    cos = np.cos(ang).astype(np.float32)
    sin = np.sin(ang).astype(np.float32)
    cosT = np.concatenate([cos, cos], 0)
    sinT = np.concatenate([-sin, sin], 0)
    return np.tile(cosT, (4, 1)).copy(), np.tile(sinT, (4, 1)).copy()


def _col(v, n):
    return np.ascontiguousarray(np.asarray(v, np.float32).reshape(n, 128).T)


def make_in_maps(inputs):
    f = lambda a: np.ascontiguousarray(np.asarray(a, np.float32))
    cosT, sinT = _rope_tables()
    shared = {
        "w_ada": f(inputs["w_ada"][0]), "b_ada": f(inputs["b_ada"][0]).reshape(1, -1),
        "gpre1": _col(inputs["g_pre_mix"][0], 8), "gpre2": _col(inputs["g_pre_ffn"][0], 8),
        "gpost1": f(inputs["g_post_mix"][0]).reshape(1, -1), "gpost2": f(inputs["g_post_ffn"][0]).reshape(1, -1),
        "w_in": f(inputs["w_in"][0]), "gq": _col(inputs["g_q"][0], 2), "w_uq": f(inputs["w_uq"][0]),
        "gkv": _col(inputs["g_kv"][0], 1), "w_ukv": f(inputs["w_ukv"][0]),
        "lng": f(inputs["gm_ln_g"][0]).reshape(1, 512), "lnb": f(inputs["gm_ln_b"][0]).reshape(1, 512),
        "wsT": np.ascontiguousarray(np.transpose(f(inputs["w_spatial"][0]), (2, 0, 1)).reshape(128, 1024)),
        "bs": np.ascontiguousarray(f(inputs["b_spatial"][0]).T),
        "w_out": f(inputs["w_out"][0]), "w_up": f(inputs["w_up"][0]),
        "convw": np.ascontiguousarray(np.transpose(f(inputs["conv_w"][0]).reshape(3, 44, 128), (2, 1, 0)).reshape(128, 132)),
        "convb": np.ascontiguousarray(f(inputs["conv_b"][0]).reshape(44, 128).T),
        "w_down": f(inputs["w_down"][0]), "cosT": cosT, "sinT": sinT,
    }
    x = f(inputs["x"])
    c = f(inputs["c"])
    maps = []
    for b in range(N_CORES):
        m = dict(shared)
        m["x"] = x[b]
        m["ccol"] = _col(c[b], 8)
        maps.append(m)
    return maps


_NC_CACHE = {}


def kernel(**inputs):
    if "nc" not in _NC_CACHE:
        _NC_CACHE["nc"] = build(8, True)
    nc = _NC_CACHE["nc"]
    in_maps = make_in_maps(inputs)
    res = run_bass_kernel_spmd(nc, in_maps, core_ids=list(range(N_CORES)))
    return np.stack([np.asarray(r["out"], np.float32) for r in res.results], 0)
```

```python
import bisect
import numpy as np
import concourse.bass as bass
import concourse.mybir as mybir
from concourse.bass_utils import run_bass_kernel_spmd

F32 = mybir.dt.float32
F32R = mybir.dt.float32r
BF16 = mybir.dt.bfloat16
ALU = mybir.AluOpType
AF = mybir.ActivationFunctionType
AX = mybir.AxisListType

ENGS = ("pe", "act", "dve", "pool", "sp")
DTSZ = {F32: 4, F32R: 4, BF16: 2}

S = 4096
D = 1024
NHEAD = 8
DFF = 2816
NCH = 22
ATTN_SCALE = 96.0 ** -0.5
EPS = 1e-6
N_CORES = 8


class View:
    __slots__ = ("ap", "reg", "extra")

    def __init__(self, ap, reg):
        self.ap = ap
        self.reg = reg
        self.extra = ()

    def re(self, pat, **kw):
        return View(self.ap.rearrange(pat, **kw), self.reg)

    def also(self, *regs):
        v = View(self.ap, self.reg)
        v.extra = list(regs)
        return v


class Root:
    def __init__(self, prog, h, P, F, kind):
        self.prog = prog
        self.h = h
        self.P = P
        self.F = F
        self.kind = kind
        self.id = len(prog.roots)
        prog.roots.append(self)
        self.recs = []

    def v(self, p0=0, p1=None, f0=0, f1=None):
        p1 = self.P if p1 is None else p1
        f1 = self.F if f1 is None else f1
        assert 0 <= p0 < p1 <= self.P and 0 <= f0 < f1 <= self.F, (p0, p1, f0, f1)
        if self.kind == "ps":
            reg = (self.id, 0, self.P, (f0 // 512) * 512, ((f1 + 511) // 512) * 512)
        else:
            reg = (self.id, p0, p1, f0, f1)
        return View(self.h[p0:p1, f0:f1], reg)


class Sub:
    def __init__(self, parent, off, F, dt):
        assert off % 4 == 0
        self.parent = parent
        self.off = off
        self.F = F
        self.dt = dt
        self.sz = DTSZ[dt]
        assert off + F * self.sz <= parent.F * 2, (off, F, self.sz, parent.F * 2)

    def v(self, p0=0, p1=128, f0=0, f1=None):
        f1 = self.F if f1 is None else f1
        assert 0 <= p0 < p1 <= 128 and 0 <= f0 < f1 <= self.F, (p0, p1, f0, f1, self.F)
        b0 = self.off + f0 * self.sz
        b1 = self.off + f1 * self.sz
        assert b0 % 2 == 0 and b1 % 2 == 0
        ap = self.parent.h[p0:p1, b0 // 2:b1 // 2]
        if self.dt != BF16:
            ap = ap.bitcast(self.dt)
        return View(ap, (self.parent.id, p0, p1, b0 // 2, b1 // 2))

    def alias(self, dt):
        return Sub(self.parent, self.off, self.F * self.sz // DTSZ[dt], dt)


class PB:
    def __init__(self, root, bank, dt=None):
        self.root = root
        self.bank = bank
        self.dt = dt or F32
        self.F = 512 if self.dt == F32 else 1024

    def v(self, p0=0, p1=128, f0=0, f1=None):
        f1 = self.F if f1 is None else f1
        base = self.root.h[p0:p1, self.bank * 512:(self.bank + 1) * 512]
        if self.dt == BF16:
            base = base.bitcast(BF16)
        return View(base[:, f0:f1], (self.root.id, 0, 128, self.bank * 512, (self.bank + 1) * 512))

    def as_bf16(self):
        return PB(self.root, self.bank, BF16)


class Op:
    __slots__ = ("eng", "fn", "reads", "writes", "tag", "idx", "deps", "inc", "semval", "waits")


class Prog:
    def __init__(self, nc):
        self.nc = nc
        self.roots = []
        self.ops = []

    def sbuf_root(self, name, ncols):
        h = self.nc.alloc_sbuf_tensor(name, [128, ncols], BF16)
        return Root(self, h, 128, ncols, "sb")

    def ps(self, name, F, dt=F32):
        h = self.nc.alloc_psum_tensor(name, [128, F], dt)
        return Root(self, h, 128, F, "ps")

    def dram(self, name, shape, dt, kind):
        h = self.nc.dram_tensor(name, list(shape), dt, kind=kind)
        R = shape[0]
        C = int(np.prod(shape[1:])) if len(shape) > 1 else 1
        return Root(self, h, R, C, "dram")

    def op(self, eng, fn, reads=(), writes=(), tag=None):
        o = Op()
        o.eng = eng
        o.fn = fn
        o.reads = [r.reg for r in reads] + [x for r in reads for x in r.extra]
        o.writes = [w.reg for w in writes] + [x for w in writes for x in w.extra]
        o.tag = tag
        o.idx = len(self.ops)
        o.deps = set()
        o.inc = False
        o.semval = 0
        o.waits = []
        self.ops.append(o)
        self._track(o)
        return o

    def dma(self, out, in_, tag, **kw):
        return self.op("sp", lambda e: e.dma_start(out=out.ap, in_=in_.ap, **kw), [in_], [out], tag=tag)

    def _track(self, o):
        ops = self.ops
        roots = self.roots
        for reg in o.reads:
            r0, r1, r2, r3 = reg[1:]
            for rec in roots[reg[0]].recs:
                if rec[5] and rec[0] < r1 and r0 < rec[1] and rec[2] < r3 and r2 < rec[3]:
                    o.deps.add(rec[4])
        for reg in o.writes:
            root = roots[reg[0]]
            r0, r1, r2, r3 = reg[1:]
            keep = []
            for rec in root.recs:
                if rec[0] < r1 and r0 < rec[1] and rec[2] < r3 and r2 < rec[3]:
                    if rec[4] != o.idx:
                        o.deps.add(rec[4])
                    if r0 <= rec[0] and rec[1] <= r1 and r2 <= rec[2] and rec[3] <= r3:
                        continue
                keep.append(rec)
            root.recs = keep
        for reg in o.reads:
            root = roots[reg[0]]
            r = reg[1:]
            if o.eng != "sp":
                root.recs = [
                    rec for rec in root.recs
                    if rec[5] or rec[:4] != r or ops[rec[4]].eng != o.eng
                ]
            root.recs.append((r[0], r[1], r[2], r[3], o.idx, False))
        for reg in o.writes:
            r = reg[1:]
            roots[reg[0]].recs.append((r[0], r[1], r[2], r[3], o.idx, True))

    def finalize(self):
        ops = self.ops
        per_eng = {e: [] for e in ENGS}
        for o in ops:
            per_eng[o.eng].append(o)
        needed = []
        for o in ops:
            best = {}
            for d in o.deps:
                p = ops[d]
                if p.eng == "pe" and o.eng == "pe":
                    continue
                key = ("tag", p.tag) if p.eng == "sp" else ("eng", p.eng)
                if key not in best or best[key] < p.idx:
                    best[key] = p.idx
            needed.append(best)
            for pidx in best.values():
                ops[pidx].inc = True
        for e in ("pe", "act", "dve", "pool"):
            c = 0
            for o in per_eng[e]:
                if o.inc:
                    c += 1
                    o.semval = c
        tag_lists = {}
        for o in per_eng["sp"]:
            tag_lists.setdefault(o.tag, []).append(o.idx)
        self.tags = sorted(tag_lists.keys())
        waited = {e: {} for e in ENGS}
        for o in ops:
            w = []
            for key, pidx in needed[o.idx].items():
                if key[0] == "eng":
                    val = ops[pidx].semval
                else:
                    val = 16 * bisect.bisect_left(tag_lists[key[1]], o.idx)
                if waited[o.eng].get(key, 0) >= val:
                    continue
                waited[o.eng][key] = val
                w.append((key, val))
            o.waits = w
        self.per_eng = per_eng
        self.tag_totals = {t: len(l) for t, l in tag_lists.items()}

    def emit(self):
        nc = self.nc
        sems = {}
        for e in ("pe", "act", "dve", "pool"):
            sems[("eng", e)] = nc.alloc_semaphore(name=f"s_{e}")
        for t in self.tags:
            sems[("tag", t)] = nc.alloc_semaphore(name=f"d_{t}")
        per_eng = self.per_eng
        totals = self.tag_totals

        def run(eng_name, e):
            for o in per_eng[eng_name]:
                for key, val in o.waits:
                    e.wait_ge(sems[key], val)
                ins = o.fn(e)
                if eng_name == "sp":
                    ins.then_inc(sems[("tag", o.tag)], 16)
                elif o.inc:
                    ins.then_inc(sems[("eng", eng_name)], 1)
            if eng_name == "sp":
                for t, c in totals.items():
                    e.wait_ge(sems[("tag", t)], 16 * c)

        with nc.Block() as block:
            @block.sync
            def _(e):
                run("sp", e)

            @block.tensor
            def _(e):
                run("pe", e)

            @block.scalar
            def _(e):
                run("act", e)

            @block.vector
            def _(e):
                run("dve", e)

            @block.gpsimd
            def _(e):
                run("pool", e)


def build(NT=8, phase2=True):
    nc = bass.Bass("TRN2", target_bir_lowering=False)
    P = Prog(nc)
    SEQ = NT * 512

    def din(name, shape):
        return P.dram(name, shape, F32, "ExternalInput")

    x_d = din("x", [S, D])
    ccol_d = din("ccol", [128, 8])
    wada_d = din("w_ada", [D, 6 * D])
    bada_d = din("b_ada", [1, 6 * D])
    gpre1_d = din("gpre1", [128, 8])
    gpre2_d = din("gpre2", [128, 8])
    gpost1_d = din("gpost1", [1, D])
    gpost2_d = din("gpost2", [1, D])
    win_d = din("w_in", [D, 1440])
    gq_d = din("gq", [128, 2])
    wuq_d = din("w_uq", [256, 768])
    gkv_d = din("gkv", [128, 1])
    wukv_d = din("w_ukv", [128, 1024])
    lng_d = din("lng", [1, 512])
    lnb_d = din("lnb", [1, 512])
    wsT_d = din("wsT", [128, 1024])
    bs_d = din("bs", [128, 8])
    wout_d = din("w_out", [D, D])
    wup_d = din("w_up", [D, 2 * DFF])
    convw_d = din("convw", [128, 132])
    convb_d = din("convb", [128, 44])
    wdn_d = din("w_down", [DFF, D])
    cos_d = din("cosT", [128, S])
    sin_d = din("sinT", [128, S])
    out_d = P.dram("out", [S, D], F32, "ExternalOutput")

    SBCOLS = 106000
    SB = P.sbuf_root("SB", SBCOLS)
    off = [0]

    def carve(F, dt, at=None):
        if at is None:
            at = off[0]
            off[0] = at + ((F * DTSZ[dt] + 3) // 4) * 4
        return Sub(SB, at, F, dt)

    KT = carve(8 * 4096, BF16)
    VA = carve(32 * 520 + 64, BF16)
    WIN = carve(8 * 1472, BF16)
    WUQ = carve(2 * 1024, BF16)
    WUKV = carve(1024, BF16)
    WOUT = carve(8 * 1024, BF16)
    WST = carve(8 * 128, BF16)
    PH2_END = 45056 + 90112
    assert off[0] >= PH2_END
    IDENT = carve(128, BF16)
    NEGH = carve(512, F32)
    GG1 = carve(1024, F32)
    GG2 = carve(1024, F32)
    COLS = carve(64, F32)
    GBC = carve(512, F32)
    CST = carve(512, F32)
    BSC = carve(8, F32)
    CONVW = carve(132, F32)
    CONVB = carve(44, F32)
    ST = carve(128, F32)
    ONESF = carve(128, F32)
    IDENTF = carve(128, F32)
    CARRY = [carve(44 * 2, F32) for _ in range(2)]
    R0 = off[0]
    RSZ = SBCOLS * 2 - R0

    def rcarve(base, F, dt):
        b = R0 + base
        assert base + F * DTSZ[dt] <= RSZ, (base, F, RSZ)
        return Sub(SB, b, F, dt)

    WDN = Sub(SB, 0, 22 * 1024, BF16)
    WUP = Sub(SB, 45056, 8 * 5632, BF16)

    RT_, RM_, RS_, RO_ = (P.ps(n, 1024) for n in ("psT", "psM", "psS", "psO"))
    PT = [PB(RT_, i, BF16) for i in range(2)]
    PM = [PB(RM_, i) for i in range(2)]
    PSS = [PB(RS_, i) for i in range(2)]
    PO = [PB(RO_, i) for i in range(2)]

    def rd(*vs):
        return [v for v in vs if isinstance(v, View)]

    def A(x):
        return x.ap if isinstance(x, View) else x

    def mm(out, lhsT, rhs, start, stop):
        P.op("pe", lambda e: e.matmul(out.ap, lhsT=lhsT.ap, rhs=rhs.ap, start=start, stop=stop), [lhsT, rhs], [out])

    def tr(out, in_):
        idv = IDENT.v(0, in_.ap.shape[0], 0, in_.ap.shape[0])
        P.op("pe", lambda e: e.transpose(out=out.ap, in_=in_.ap, identity=idv.ap), [in_, idv], [out])

    def act(out, in_, func, scale=1.0, bias=None, accum=None):
        kw = {}
        if bias is not None:
            kw["bias"] = A(bias)
        if accum is not None:
            kw["accum_out"] = accum.ap
        P.op("act", lambda e: e.activation(out=out.ap, in_=in_.ap, func=func, scale=A(scale), **kw),
             [in_] + rd(scale, bias), [out] + rd(accum))

    def ts(eng, out, in0, s1, s2, op0, op1=None):
        kw = {} if op1 is None else {"op1": op1}
        P.op(eng, lambda e: e.tensor_scalar(out=out.ap, in0=in0.ap, scalar1=A(s1), scalar2=A(s2), op0=op0, **kw),
             [in0] + rd(s1, s2), [out])

    def tt(eng, out, in0, in1, op):
        P.op(eng, lambda e: e.tensor_tensor(out=out.ap, in0=in0.ap, in1=in1.ap, op=op), [in0, in1], [out])

    def stt(out, in0, scalar, in1, op0, op1):
        P.op("dve", lambda e: e.scalar_tensor_tensor(out=out.ap, in0=in0.ap, scalar=A(scalar), in1=in1.ap, op0=op0, op1=op1),
             [in0, in1] + rd(scalar), [out])

    def cp(eng, out, in_):
        if eng == "act":
            P.op("act", lambda e: e.activation(out=out.ap, in_=in_.ap, func=AF.Copy), [in_], [out])
        else:
            P.op(eng, lambda e: e.tensor_copy(out=out.ap, in_=in_.ap), [in_], [out])

    def memset(eng, out, val):
        P.op(eng, lambda e: e.memset(out.ap, val), [], [out])

    def red(out, in_, op=ALU.add):
        P.op("dve", lambda e: e.tensor_reduce(out=out.ap, in_=in_.ap, axis=AX.X, op=op), [in_], [out])

    def recip(out, in_):
        P.op("dve", lambda e: e.reciprocal(out=out.ap, in_=in_.ap), [in_], [out])

    def bview(v, pat_shape):
        return View(v.ap.to_broadcast(pat_shape), v.reg)

    def rstd_from(out, ss, n, width):
        ts("dve", out, ss, 1.0 / n, EPS, ALU.mult, ALU.add)
        p0, p1 = out.reg[1], out.reg[2]
        tt("pool", out, out, NEGH.v(p0, p1, 0, width), ALU.pow)

    memset("pool", NEGH.v(), -0.5)
    memset("pool", ONESF.v(), 1.0)
    identf = IDENTF
    memset("pool", identf.v(), 1.0)
    P.op("pool", lambda e: e.affine_select(out=identf.v().ap, in_=identf.v().ap, pattern=[[-1, 128]],
                                            compare_op=ALU.is_equal, fill=0.0, base=0, channel_multiplier=1),
         [identf.v()], [identf.v()])
    cp("dve", IDENT.v(), identf.v())
    memset("pool", CARRY[0].v(), 0.0)

    ccol = rcarve(512, 8, F32)
    P.dma(ccol.v(), ccol_d.v(), "c0")
    P.dma(COLS.v(0, 128, 32, 34), gq_d.v(), "c0")
    P.dma(COLS.v(0, 128, 34, 35), gkv_d.v(), "c0")
    P.dma(BSC.v(), bs_d.v(), "c0")
    P.dma(CONVW.v(), convw_d.v(), "c0")
    P.dma(CONVB.v(), convb_d.v(), "c0")
    gpre = rcarve(544, 16, F32)
    P.dma(gpre.v(0, 128, 0, 8), gpre1_d.v(), "c0")
    P.dma(gpre.v(0, 128, 8, 16), gpre2_d.v(), "c0")

    cact = rcarve(608, 8, F32)
    act(cact.v(), ccol.v(), AF.Silu)
    crep = rcarve(1024, 8 * 128, F32)
    for kc in range(8):
        ts("dve", crep.v(0, 128, kc * 128, (kc + 1) * 128), identf.v(), 0.0, cact.v(0, 128, kc, kc + 1), ALU.mult, ALU.add)
    ADA = KT.alias(F32)
    BADA = Sub(SB, KT.off + 6144 * 4, 6144, F32)
    P.dma(BADA.v(), View(bada_d.h[0:1, :].partition_broadcast(128), bada_d.v().reg), "c0")
    wst = [rcarve(8192 + i * 16384, 8 * 512, F32) for i in range(2)] + [Sub(SB, VA.off + i * 16384, 8 * 512, F32) for i in range(2)]
    for n in range(12):
        st_ = wst[n % 4]
        P.dma(st_.v().re("p (c n) -> p c n", c=8),
              View(wada_d.h[:, n * 512:(n + 1) * 512].rearrange("(c p) n -> p c n", p=128), wada_d.v(0, D, n * 512, (n + 1) * 512).reg),
              f"wa{n % 4}")
        pm = PM[n % 2]
        for kc in range(8):
            mm(pm.v(), crep.v(0, 128, kc * 128, (kc + 1) * 128), st_.v(0, 128, kc * 512, (kc + 1) * 512), kc == 0, kc == 7)
        tt("dve", ADA.v(0, 128, n * 512, (n + 1) * 512), pm.v(), BADA.v(0, 128, n * 512, (n + 1) * 512), ALU.add)

    memset("pool", VA.v(0, 128, 32 * 520, 32 * 520 + 64), 0.0)
    P.op("pool", lambda e: e.memset(VA.v(0, 128, 0, 32 * 520).ap.rearrange("p (k c) -> p k c", c=65)[:, :, 64:65], 1.0), [], [VA.v(0, 128, 0, 32 * 520)])
    tmpx = rcarve(8192, 1024, F32)
    colx = rcarve(640, 32, F32)
    for i, base in enumerate((0, 1024, 3072, 4096)):
        P.op("dve", lambda e, base=base: e.tensor_tensor(
            out=tmpx.v().ap.rearrange("p (j n) -> p j n", j=8),
            in0=ADA.v(0, 128, base, base + 1024).ap.rearrange("p (j n) -> p j n", j=8),
            in1=identf.v().ap.unsqueeze(1).to_broadcast([128, 8, 128]), op=ALU.mult),
            [ADA.v(0, 128, base, base + 1024), identf.v()], [tmpx.v()])
        red(colx.v(0, 128, i * 8, (i + 1) * 8), tmpx.v().re("p (j n) -> p j n", j=8))
    ts("dve", colx.v(0, 128, 8, 16), colx.v(0, 128, 8, 16), 1.0, None, ALU.add)
    ts("dve", colx.v(0, 128, 24, 32), colx.v(0, 128, 24, 32), 1.0, None, ALU.add)
    tt("dve", COLS.v(0, 128, 0, 8), colx.v(0, 128, 8, 16), gpre.v(0, 128, 0, 8), ALU.mult)
    tt("dve", COLS.v(0, 128, 16, 24), colx.v(0, 128, 24, 32), gpre.v(0, 128, 8, 16), ALU.mult)
    cp("dve", COLS.v(0, 128, 8, 16), colx.v(0, 128, 0, 8))
    cp("dve", COLS.v(0, 128, 24, 32), colx.v(0, 128, 16, 24))
    P.dma(GG1.v(), View(gpost1_d.h[0:1, :].partition_broadcast(128), gpost1_d.v().reg), "c0")
    P.dma(GG2.v(), View(gpost2_d.h[0:1, :].partition_broadcast(128), gpost2_d.v().reg), "c0")
    tt("dve", GG1.v(), GG1.v(), ADA.v(0, 128, 2048, 3072), ALU.mult)
    tt("dve", GG2.v(), GG2.v(), ADA.v(0, 128, 5120, 6144), ALU.mult)

    P.dma(GBC.v(), View(lng_d.h[0:1, :].partition_broadcast(128), lng_d.v().reg), "c0")
    P.dma(CST.v(), View(lnb_d.h[0:1, :].partition_broadcast(128), lnb_d.v().reg), "c0")

    wstage = [rcarve(8192 + i * 16384, 4096, F32) for i in range(2)]
    cast_engs = ["dve", "act", "dve"]
    cnt = [0]

    def stage_load(src_view_ap, src_reg, ncols):
        st_ = wstage[cnt[0] % 2]
        tagn = f"ws{cnt[0] % 2}"
        cnt[0] += 1
        P.dma(st_.v(0, 128, 0, ncols), View(src_view_ap, src_reg), tagn)
        return st_

    def cast(dst, src):
        eng = cast_engs[cnt[0] % 3]
        cp(eng, dst, src)

    for kc in range(8):
        st_ = stage_load(win_d.h[kc * 128:(kc + 1) * 128, :], win_d.v(kc * 128, (kc + 1) * 128).reg, 1440)
        b = kc * 1472
        cp("dve", WIN.v(0, 128, b, b + 416), st_.v(0, 128, 0, 416))
        cp("pool", WIN.v(0, 128, b + 416, b + 432), st_.v(0, 128, 400, 416))
        cp("pool", WIN.v(0, 128, b + 432, b + 448), st_.v(0, 128, 384, 400))
        cp("act", WIN.v(0, 128, b + 448, b + 1472), st_.v(0, 128, 416, 1440))
    for kc in range(2):
        st_ = stage_load(wuq_d.h[kc * 128:(kc + 1) * 128, :], wuq_d.v(kc * 128, (kc + 1) * 128).reg, 768)
        b = kc * 1024
        gq = COLS.v(0, 128, 32 + kc, 33 + kc)
        s3 = st_.v(0, 128, 0, 768).re("p (h c) -> p h c", h=8)
        P.op("dve", lambda e, b=b, s3=s3, gq=gq: e.tensor_scalar(
            out=WUQ.v(0, 128, b, b + 512).ap.rearrange("p (h c) -> p h c", h=8), in0=s3.ap[:, :, 0:64],
            scalar1=gq.ap, scalar2=None, op0=ALU.mult), [s3, gq], [WUQ.v(0, 128, b, b + 512)])
        P.op("dve", lambda e, b=b, s3=s3, gq=gq: e.tensor_scalar(
            out=WUQ.v(0, 128, b + 512, b + 768).ap.rearrange("p (h c) -> p h c", h=8), in0=s3.ap[:, :, 64:96],
            scalar1=gq.ap, scalar2=None, op0=ALU.mult), [s3, gq], [WUQ.v(0, 128, b + 512, b + 768)])
        P.op("dve", lambda e, b=b, s3=s3, gq=gq: e.tensor_scalar(
            out=WUQ.v(0, 128, b + 768, b + 1024).ap.rearrange("p (h c) -> p h c", h=8)[:, :, 0:16], in0=s3.ap[:, :, 80:96],
            scalar1=gq.ap, scalar2=None, op0=ALU.mult), [s3, gq], [WUQ.v(0, 128, b + 768, b + 1024)])
        P.op("dve", lambda e, b=b, s3=s3, gq=gq: e.tensor_scalar(
            out=WUQ.v(0, 128, b + 768, b + 1024).ap.rearrange("p (h c) -> p h c", h=8)[:, :, 16:32], in0=s3.ap[:, :, 64:80],
            scalar1=gq.ap, scalar2=None, op0=ALU.mult), [s3, gq], [WUQ.v(0, 128, b + 768, b + 1024)])
    st_ = stage_load(wukv_d.h[:, :], wukv_d.v().reg, 1024)
    gkv = COLS.v(0, 128, 34, 35)
    s3 = st_.v(0, 128, 0, 1024).re("p (h c) -> p h c", h=8)
    P.op("dve", lambda e, s3=s3: e.tensor_scalar(
        out=WUKV.v(0, 128, 0, 512).ap.rearrange("p (h c) -> p h c", h=8), in0=s3.ap[:, :, 0:64],
        scalar1=gkv.ap, scalar2=None, op0=ALU.mult), [s3, gkv], [WUKV.v(0, 128, 0, 512)])
    P.op("dve", lambda e, s3=s3: e.tensor_scalar(
        out=WUKV.v(0, 128, 512, 1024).ap.rearrange("p (h c) -> p h c", h=8), in0=s3.ap[:, :, 64:128],
        scalar1=gkv.ap, scalar2=None, op0=ALU.mult), [s3, gkv], [WUKV.v(0, 128, 512, 1024)])
    for kc in range(8):
        st_ = stage_load(wout_d.h[kc * 128:(kc + 1) * 128, :], wout_d.v(kc * 128, (kc + 1) * 128).reg, 1024)
        cast(WOUT.v(0, 128, kc * 1024, (kc + 1) * 1024), st_.v(0, 128, 0, 1024))
    st_ = stage_load(wsT_d.h[:, :], wsT_d.v().reg, 1024)
    P.op("pool", lambda e, st_=st_: e.memset(st_.v(64, 128, 0, 1024).ap.rearrange("p (h i) -> p h i", h=8)[:, :, 0:64], 0.0),
         [], [st_.v(64, 128, 0, 1024)])
    cp("dve", WST.v(), st_.v(0, 128, 0, 1024))
    onesb = rcarve(768, 8, BF16)
    memset("pool", onesb.v(), 1.0)
    rs = rcarve(800, 8, F32)
    for h in range(8):
        mm(PM[0].v(0, 128, h, h + 1), WST.v(0, 128, h * 128, (h + 1) * 128), onesb.v(0, 128, 0, 1), True, True)
    cp("dve", rs.v(), PM[0].v(0, 128, 0, 8))
    for h in range(8):
        ts("dve", CST.v(0, 128, h * 64, (h + 1) * 64), CST.v(0, 128, h * 64, (h + 1) * 64),
           rs.v(0, 128, h, h + 1), BSC.v(0, 128, h, h + 1), ALU.mult, ALU.add)

    HT = rcarve(0, 8 * 512, BF16)
    QT = rcarve(8192, 8 * 512, BF16)
    SGUT = rcarve(16384, 4 * 512, BF16)
    AT = rcarve(20480, 4 * 512, BF16)
    U0 = 24576
    XIN = [rcarve(U0 + 16384, 1024, F32), rcarve(U0 + 12288, 1024, F32)]
    XN = rcarve(0, 4096, BF16)
    CQF = rcarve(U0, 3 * 512, F32)
    SQ = rcarve(U0 + 6144, 2048, F32)
    DIAG = rcarve(U0 + 14336, 1024, F32)
    CQN = rcarve(U0 + 18432, 3 * 512, BF16)
    TMP1 = rcarve(U0, 512, F32)
    TMP2 = rcarve(U0 + 2048, 512, F32)
    ROT = rcarve(U0 + 4096, 512, BF16)
    CS = rcarve(U0 + 6144, 1024, F32)
    GBs = [rcarve(U0 + i * 10240, 1024, F32) for i in range(2)]
    SQ2s = [rcarve(U0 + i * 10240 + 4096, 512, F32) for i in range(2)]
    VNs = [rcarve(U0 + i * 10240 + 6144, 512, BF16) for i in range(2)]
    MIXs = [rcarve(U0 + i * 10240 + 7168, 512, F32) for i in range(2)]
    SGUs = [rcarve(U0 + i * 10240 + 9216, 512, BF16) for i in range(2)]
    PTBP = [rcarve(U0, 1024, BF16), rcarve(U0 + 2048, 1024, BF16), rcarve(U0 + 20480, 1024, BF16)]
    RDEN = [rcarve(U0 + 4096 + i * 2048, 512, F32) for i in range(2)]
    RDEN2 = [rcarve(U0 + 8192 + i * 2048, 512, F32) for i in range(2)]
    XRES = [rcarve(U0 + i * 4096, 1024, F32) for i in range(2)]
    TT_ = [rcarve(U0 + 8192 + i * 4096, 1024, F32) for i in range(2)]

    def tbank(fc, banks=None):
        banks = banks or [PT[0], PT[1], PM[0].as_bf16(), PM[1].as_bf16()]
        pb = banks[fc // 2]
        return lambda c0, c1: pb.v(0, 128, c0, c1)

    def prep_a(src_d, t0, XINb, XNb):
        for sbk in range(4):
            r0 = t0 + sbk * 128
            xin = XINb[sbk % 2]
            xn = XNb.v(0, 128, sbk * 1024, (sbk + 1) * 1024)
            P.dma(xin.v(), src_d.v(r0, r0 + 128), f"xin{sbk % 2}")
            act(xn, xin.v(), AF.Square, accum=ST.v(0, 128, 2 * sbk, 2 * sbk + 1))
            rstd_from(ST.v(0, 128, 2 * sbk + 1, 2 * sbk + 2), ST.v(0, 128, 2 * sbk, 2 * sbk + 1), D, 1)
            ts("dve", xn, xin.v(), ST.v(0, 128, 2 * sbk + 1, 2 * sbk + 2), None, ALU.mult)

    def prep_b(gmod_c, sh_c, HTbuf, XNb, banks=None):
        for fc in range(8):
            bank = tbank(fc, banks)
            for sbk in range(4):
                c0 = (fc % 2) * 512 + sbk * 128
                tr(bank(c0, c0 + 128), XNb.v(0, 128, sbk * 1024 + fc * 128, sbk * 1024 + (fc + 1) * 128))
        for fc in range(8):
            bank = tbank(fc, banks)
            c0 = (fc % 2) * 512
            act(HTbuf.v(0, 128, fc * 512, (fc + 1) * 512), bank(c0, c0 + 512), AF.Identity,
                scale=COLS.v(0, 128, gmod_c + fc, gmod_c + fc + 1), bias=COLS.v(0, 128, sh_c + fc, sh_c + fc + 1))

    def prep_hT(src_d, t0, gmod_c, sh_c, HTbuf, XINb, XNb):
        prep_a(src_d, t0, XINb, XNb)
        prep_b(gmod_c, sh_c, HTbuf, XNb)

    def post_norm_residual(pm_pair, src_d, dst_d, r0, GG, slot, XRESb, TTb):
        xr = XRESb[slot]
        tb = TTb[slot]
        tbj = tb.alias(BF16)
        s0 = 8 + 4 * slot
        P.dma(xr.v(), src_d.v(r0, r0 + 128), f"xr{slot}")
        for hf in range(2):
            act(tbj.v(0, 128, hf * 512, (hf + 1) * 512), pm_pair[hf].v(), AF.Square, accum=ST.v(0, 128, s0 + hf, s0 + hf + 1))
        tt("dve", ST.v(0, 128, s0 + 2, s0 + 3), ST.v(0, 128, s0, s0 + 1), ST.v(0, 128, s0 + 1, s0 + 2), ALU.add)
        rstd_from(ST.v(0, 128, s0 + 3, s0 + 4), ST.v(0, 128, s0 + 2, s0 + 3), D, 1)
        for hf in range(2):
            tt("dve", tb.v(0, 128, hf * 512, (hf + 1) * 512), pm_pair[hf].v(), GG.v(0, 128, hf * 512, (hf + 1) * 512), ALU.mult)
        stt(xr.v(), tb.v(), ST.v(0, 128, s0 + 3, s0 + 4), xr.v(), ALU.mult, ALU.add)
        P.dma(dst_d.v(r0, r0 + 128), xr.v(), f"xo{slot}")

    for T in range(NT):
        t0 = T * 512
        if T == 0:
            prep_a(x_d, t0, XIN, XN)
        if T == 0:
            prep_b(0, 8, HT, XN)
        for c in range(3):
            pm = PM[c % 2]
            for kc in range(8):
                mm(pm.v(), WIN.v(0, 128, kc * 1472 + c * 128, kc * 1472 + (c + 1) * 128), HT.v(0, 128, kc * 512, (kc + 1) * 512), kc == 0, kc == 7)
            cp("act", CQF.v(0, 128, c * 512, (c + 1) * 512), pm.v())
        def norm_steps(grp, chunks, nfeat):
            pss = PSS[grp]
            pbc = PM[grp]
            sc0 = 40 + grp * 4

            def squares():
                for i, c in enumerate(chunks):
                    act(SQ.v(0, 128, (grp * 2 + i) * 512, (grp * 2 + i + 1) * 512), CQF.v(0, 128, c * 512, (c + 1) * 512), AF.Square)

            def sums():
                for sbk in range(4):
                    for i, c in enumerate(chunks):
                        b = (grp * 2 + i) * 512 + sbk * 128
                        mm(pss.v(0, 128, sbk, sbk + 1), SQ.v(0, 128, b, b + 128), ONESF.v(0, 128, 0, 1), i == 0, i == len(chunks) - 1)

            def bcast():
                for sbk in range(4):
                    ts("dve", DIAG.v(0, 128, grp * 512 + sbk * 128, grp * 512 + (sbk + 1) * 128), IDENTF.v(), ST.v(0, 128, sc0 + sbk, sc0 + sbk + 1), None, ALU.mult)
                    mm(pbc.v(0, 128, sbk * 128, (sbk + 1) * 128), ONESF.v(), DIAG.v(0, 128, grp * 512 + sbk * 128, grp * 512 + (sbk + 1) * 128), True, True)

            def apply():
                for c in chunks:
                    tt("dve", CQN.v(0, 128, c * 512, (c + 1) * 512), CQF.v(0, 128, c * 512, (c + 1) * 512), pbc.v(), ALU.mult)

            return [
                squares,
                sums,
                lambda: ts("dve", ST.v(0, 128, sc0, sc0 + 4), pss.v(0, 128, 0, 4), 1.0 / nfeat, EPS, ALU.mult, ALU.add),
                lambda: tt("pool", ST.v(0, 128, sc0, sc0 + 4), ST.v(0, 128, sc0, sc0 + 4), NEGH.v(0, 128, 0, 4), ALU.pow),
                bcast,
                apply,
            ]

        for fa, fb in zip(norm_steps(0, (0, 1), 256), norm_steps(1, (2,), 128)):
            fa()
            fb()
        P.dma(CS.v(0, 128, 0, 512), cos_d.v(0, 128, t0, t0 + 512), "cs")
        P.dma(CS.v(0, 128, 512, 1024), sin_d.v(0, 128, t0, t0 + 512), "cs")
        for sw in range(2):
            pm = PM[sw]
            for kc in range(8):
                b = kc * 1472 + 384 + sw * 32
                mm(pm.v(64, 96), WIN.v(0, 128, b, b + 32), HT.v(0, 128, kc * 512, (kc + 1) * 512), kc == 0, kc == 7)
        tt("dve", TMP1.v(64, 96), PM[0].v(64, 96), CS.v(64, 96, 0, 512), ALU.mult)
        tt("dve", TMP2.v(64, 96), PM[1].v(64, 96), CS.v(64, 96, 512, 1024), ALU.mult)
        tt("pool", ROT.v(64, 96), TMP1.v(64, 96), TMP2.v(64, 96), ALU.add)
        for h in range(8):
            cp("dve", KT.v(64, 96, h * 4096 + t0, h * 4096 + t0 + 512), ROT.v(64, 96))
        for g in range(2):
            pr, psw = PSS[0], PSS[1]
            for kc in range(2):
                b = kc * 1024 + 512 + g * 128
                mm(pr.v(), WUQ.v(0, 128, b, b + 128), CQN.v(0, 128, kc * 512, (kc + 1) * 512), kc == 0, kc == 1)
            for kc in range(2):
                b = kc * 1024 + 768 + g * 128
                mm(psw.v(), WUQ.v(0, 128, b, b + 128), CQN.v(0, 128, kc * 512, (kc + 1) * 512), kc == 0, kc == 1)
            tt("dve", TMP1.v(), pr.v(), CS.v(0, 128, 0, 512), ALU.mult)
            tt("dve", TMP2.v(), psw.v(), CS.v(0, 128, 512, 1024), ALU.mult)
            tt("pool", ROT.v(), TMP1.v(), TMP2.v(), ALU.add)
            for hh in range(4):
                h = g * 4 + hh
                cp(("dve", "act")[hh % 2], QT.v(64, 96, h * 512, (h + 1) * 512), ROT.v(32 * hh, 32 * hh + 32))
            for hp in range(2 * g, 2 * g + 2):
                pa = PM[hp % 2]
                for kc in range(2):
                    b = kc * 1024 + hp * 128
                    mm(pa.v(), WUQ.v(0, 128, b, b + 128), CQN.v(0, 128, kc * 512, (kc + 1) * 512), kc == 0, kc == 1)
                cp("act", QT.v(0, 64, (2 * hp) * 512, (2 * hp + 1) * 512), pa.v(0, 64))
                cp("dve", QT.v(0, 64, (2 * hp + 1) * 512, (2 * hp + 2) * 512), pa.v(64, 128))
        for h in range(8):
            pk = PO[h % 2]
            mm(pk.v(0, 64), WUKV.v(0, 128, h * 64, (h + 1) * 64), CQN.v(0, 128, 1024, 1536), True, True)
            cp(("act", "dve")[h % 2], KT.v(0, 64, h * 4096 + t0, h * 4096 + t0 + 512), pk.v(0, 64))
        for sbk in range(4):
            kt = T * 4 + sbk
            pv = PO[sbk % 2]
            mm(pv.v(), CQN.v(0, 128, 1024 + sbk * 128, 1024 + (sbk + 1) * 128), WUKV.v(0, 128, 512, 1024), True, True)
            P.op("dve", lambda e, kt=kt, pv=pv: e.tensor_copy(
                out=VA.v(0, 128, kt * 520, (kt + 1) * 520).ap.rearrange("p (h c) -> p h c", c=65)[:, :, 0:64],
                in_=pv.v().ap.rearrange("p (h c) -> p h c", c=64)), [pv.v()], [VA.v(0, 128, kt * 520, (kt + 1) * 520)])
        def gm_Gmm(sbk):
            i2 = sbk % 2
            gbank = PM if i2 == 0 else PO
            for hf in range(2):
                pm = gbank[hf]
                for kc in range(8):
                    b = kc * 1472 + 448 + hf * 512
                    mm(pm.v(), HT.v(0, 128, kc * 512 + sbk * 128, kc * 512 + (sbk + 1) * 128), WIN.v(0, 128, b, b + 512), kc == 0, kc == 7)

        def gm_Gact(sbk):
            i2 = sbk % 2
            gbank = PM if i2 == 0 else PO
            for hf in range(2):
                act(GBs[i2].v(0, 128, hf * 512, (hf + 1) * 512), gbank[hf].v(), AF.Gelu_apprx_tanh)

        def gm_S(sbk):
            i2 = sbk % 2
            GB, SQ2, VN, MIX, SGU = GBs[i2], SQ2s[i2], VNs[i2], MIXs[i2], SGUs[i2]
            sb0 = 16 if i2 == 0 else 64
            sA, sB, sC = sb0, sb0 + 8, sb0 + 16
            vv = GB.v(0, 128, 512, 1024)
            uu = GB.v(0, 128, 0, 512)
            h3 = lambda v: v.re("p (h d) -> p h d", h=8)
            mean_b = View(ST.v(0, 128, sA, sA + 8).ap.unsqueeze(2).to_broadcast([128, 8, 64]), ST.v(0, 128, sA, sA + 8).reg)
            rstd_b = View(ST.v(0, 128, sB, sB + 8).ap.unsqueeze(2).to_broadcast([128, 8, 64]), ST.v(0, 128, sB, sB + 8).reg)
            pm = PSS[sbk % 2]
            pt = PT[sbk % 2]

            def spatial():
                for h in range(8):
                    mm(pm.v(0, 128, h * 64, (h + 1) * 64), WST.v(0, 128, h * 128, (h + 1) * 128), VN.v(0, 128, h * 64, (h + 1) * 64), True, True)

            def transposes():
                for c in range(4):
                    tr(pt.v(0, 128, c * 128, (c + 1) * 128), SGU.v(0, 128, c * 128, (c + 1) * 128))

            return [
                lambda: red(ST.v(0, 128, sA, sA + 8), h3(vv)),
                lambda: act(SQ2.v(), vv, AF.Square),
                lambda: red(ST.v(0, 128, sB, sB + 8), h3(SQ2.v())),
                lambda: ts("dve", ST.v(0, 128, sA, sA + 8), ST.v(0, 128, sA, sA + 8), 1.0 / 64, None, ALU.mult),
                lambda: tt("dve", ST.v(0, 128, sC, sC + 8), ST.v(0, 128, sA, sA + 8), ST.v(0, 128, sA, sA + 8), ALU.mult),
                lambda: stt(ST.v(0, 128, sB, sB + 8), ST.v(0, 128, sB, sB + 8), 1.0 / 64, ST.v(0, 128, sC, sC + 8), ALU.mult, ALU.subtract),
                lambda: ts("dve", ST.v(0, 128, sB, sB + 8), ST.v(0, 128, sB, sB + 8), EPS, None, ALU.add),
                lambda: tt("pool", ST.v(0, 128, sB, sB + 8), ST.v(0, 128, sB, sB + 8), NEGH.v(0, 128, 0, 8), ALU.pow),
                lambda: tt("dve", h3(SQ2.v()), h3(vv), mean_b, ALU.subtract),
                lambda: tt("dve", h3(VN.v()), h3(SQ2.v()), rstd_b, ALU.mult),
                spatial,
                lambda: tt("dve", MIX.v(), pm.v(), GBC.v(), ALU.mult),
                lambda: tt("dve", MIX.v(), MIX.v(), CST.v(), ALU.add),
                lambda: tt("pool", SGU.v(), MIX.v(), uu, ALU.mult),
                transposes,
                lambda: P.op("act", lambda e: e.activation(
                    out=SGUT.v().ap.rearrange("p (c t) -> p c t", c=4)[:, :, sbk * 128:(sbk + 1) * 128],
                    in_=pt.v(0, 128, 0, 512).ap.rearrange("p (c t) -> p c t", c=4), func=AF.Copy),
                    [pt.v()], [SGUT.v()]),
            ]

        def lockstep(a, b):
            for fa, fb in zip(a, b):
                fa()
                fb()

        gm_Gmm(0)
        gm_Gact(0)
        gm_Gmm(1)
        gm_Gact(1)
        gm_Gmm(2)
        gm_Gmm(3)
        lockstep(gm_S(0), gm_S(1))
        gm_Gact(2)
        gm_Gact(3)
        lockstep(gm_S(2), gm_S(3))

        nkt = 4 * T + 4
        pairs = [(h, j) for h in range(8) for j in range(nkt // 2)]
        SROOT = [RS_, RT_]

        def geom(kt):
            r = kt - 4 * T
            q0 = 128 * r if r > 0 else 0
            return r, q0, 512 - q0

        def qk(g):
            h, j = pairs[g]
            for e in range(2):
                kt = 2 * j + e
                r, q0, n = geom(kt)
                mm(PB(SROOT[g % 2], e).v(0, 128, 0, n), KT.v(0, 96, h * 4096 + kt * 128, h * 4096 + (kt + 1) * 128),
                   QT.v(0, 96, h * 512 + q0, (h + 1) * 512), True, True)

        def do_exp(g):
            h, j = pairs[g]
            sr = SROOT[g % 2]
            ptb = PTBP[g % 3]
            ge = [geom(2 * j + e) for e in range(2)]
            if ge[0][2] == 512 and ge[1][2] == 512:
                act(ptb.v(0, 128, 0, 1024), View(sr.h[:, 0:1024], (sr.id, 0, 128, 0, 1024)), AF.Exp, scale=ATTN_SCALE)
            else:
                for e in range(2):
                    n = ge[e][2]
                    act(ptb.v(0, 128, e * 512, e * 512 + n), PB(sr, e).v(0, 128, 0, n), AF.Exp, scale=ATTN_SCALE)
            for e in range(2):
                if ge[e][0] >= 0:
                    memset("pool", ptb.v(64, 128, e * 512, e * 512 + 64), 0.0)

        def do_pv(g):
            h, j = pairs[g]
            ptb = PTBP[g % 3]
            po = PO[h % 2]
            for e in range(2):
                kt = 2 * j + e
                r, q0, n = geom(kt)
                voff = kt * 520 + h * 65
                mm(po.v(0, 128, q0, 512), VA.v(0, 128, voff, voff + 128), ptb.v(0, 128, e * 512, e * 512 + n), kt == 0, kt == nkt - 1)
            if h == 1 and j == nkt // 2 - 1 and T + 1 < NT:
                prep_a(x_d, t0 + 512, XIN, XN)
            if j == nkt // 2 - 1:
                rd_ = RDEN[h % 2]
                c, half = h // 2, h % 2
                p0 = half * 64
                recip(rd_.v(64, 65), po.v(64, 65))
                pb = PM[h % 2]
                mm(pb.v(p0, p0 + 64), ONESF.v(64, 65, 0, 64), rd_.v(64, 65), True, True)
                bc = rd_.v(0, 64) if p0 == 0 else RDEN2[h % 2].v(64, 128)
                cp("act", bc, pb.v(p0, p0 + 64))
                tt("dve", AT.v(p0, p0 + 64, c * 512, (c + 1) * 512), po.v(0, 64), bc, ALU.mult)

        qk(0)
        for g in range(len(pairs) + 1):
            if g + 1 < len(pairs):
                qk(g + 1)
            if g < len(pairs):
                do_exp(g)
            if g >= 1:
                do_pv(g - 1)

        for sbk in range(4):
            pair = (PM[0], PM[1]) if sbk % 2 == 0 else (PSS[0], PSS[1])
            for hf in range(2):
                for kc in range(8):
                    src = AT if kc < 4 else SGUT
                    cc = kc % 4
                    mm(pair[hf].v(), src.v(0, 128, cc * 512 + sbk * 128, cc * 512 + (sbk + 1) * 128),
                       WOUT.v(0, 128, kc * 1024 + hf * 512, kc * 1024 + (hf + 1) * 512), kc == 0, kc == 7)
            post_norm_residual(pair, x_d, out_d, t0 + sbk * 128, GG1, sbk % 2, XRES, TT_)
            if sbk == 1 and T + 1 < NT:
                prep_b(0, 8, HT, XN, [PT[0], PT[1], PO[0].as_bf16(), PO[1].as_bf16()])

    if not phase2:
        P.finalize()
        P.emit()
        return nc

    H2T = rcarve(0, 8 * 512, BF16)
    ACTT = rcarve(8192, 22 * 512, BF16)
    XRES2 = [rcarve(30720 + i * 4096, 1024, F32) for i in range(2)]
    TT2 = [rcarve(38912 + i * 4096, 1024, F32) for i in range(2)]
    XIN2 = [Sub(SB, PH2_END + i * 4096, 1024, F32) for i in range(2)]
    XN2 = rcarve(0, 4096, BF16)
    YAB = [[Sub(SB, PH2_END + (ab * 2 + i) * 2048, 512, F32) for i in range(2)] for ab in range(2)]
    assert PH2_END + 8192 <= IDENT.off, (PH2_END, IDENT.off)
    st2 = [rcarve(30720 + i * 4096, 1024, F32) for i in range(3)]
    k2 = [0]

    def load_wdn(c):
        s_ = st2[k2[0] % 3]
        P.dma(s_.v(), wdn_d.v(c * 128, (c + 1) * 128), f"w2{k2[0] % 3}")
        cp(("dve", "act")[k2[0] % 2], WDN.v(0, 128, c * 1024, (c + 1) * 1024), s_.v())
        k2[0] += 1

    def load_wup(c, ab):
        s_ = st2[k2[0] % 3]
        col = ab * DFF + c * 128
        P.dma(s_.v().re("p (k n) -> p k n", k=8),
              View(wup_d.h[:, col:col + 128].rearrange("(k p) n -> p k n", p=128), wup_d.v(0, D, col, col + 128).reg),
              f"w2{k2[0] % 3}")
        base = WUP.off // 2 + col
        dst = View(SB.h[:, base:base + 8 * 5632].rearrange("p (k n) -> p k n", n=5632)[:, :, 0:128],
                   WUP.v(0, 128, col, col + 128).reg).also(*[WUP.v(0, 128, kc * 5632 + col, kc * 5632 + col + 128).reg for kc in range(1, 8)])
        eng = ("act", "pool")[k2[0] % 2]
        src = s_.v().re("p (k n) -> p k n", k=8)
        if eng == "act":
            P.op("act", lambda e: e.activation(out=dst.ap, in_=src.ap, func=AF.Copy), [src], [dst])
        else:
            P.op("pool", lambda e: e.tensor_copy(out=dst.ap, in_=src.ap), [src], [dst])
        k2[0] += 1

    prep_a(out_d, 0, XIN2, XN2)
    prep_b(16, 24, H2T, XN2)
    for c in range(NCH):
        load_wdn(c)
    LA = 2
    RSM0, RSM1 = PM[0].as_bf16(), PM[1].as_bf16()

    for T in range(NT):
        t0 = T * 512
        cold = CARRY[T % 2]
        cnew = CARRY[(T + 1) % 2]
        if T == 0:
            for c in range(LA):
                load_wup(c, 0)
                load_wup(c, 1)
        for c in range(NCH):
            if T == 0 and c + LA < NCH:
                load_wup(c + LA, 0)
                load_wup(c + LA, 1)
            ys = []
            for ab in range(2):
                pm = (PM, PSS, PO)[c % 3][ab]
                col = ab * DFF + c * 128
                for kc in range(8):
                    mm(pm.v(), WUP.v(0, 128, kc * 5632 + col, kc * 5632 + col + 128), H2T.v(0, 128, kc * 512, (kc + 1) * 512), kc == 0, kc == 7)
                ci = ab * NCH + c
                y = YAB[ab][c % 2]
                ys.append(y)
                w = lambda kk, ci=ci: CONVW.v(0, 128, ci * 3 + kk, ci * 3 + kk + 1)
                act(y.v(), pm.v(), AF.Identity, scale=w(2), bias=CONVB.v(0, 128, ci, ci + 1))
                stt(y.v(0, 128, 1, 512), pm.v(0, 128, 0, 511), w(1), y.v(0, 128, 1, 512), ALU.mult, ALU.add)
                stt(y.v(0, 128, 2, 512), pm.v(0, 128, 0, 510), w(0), y.v(0, 128, 2, 512), ALU.mult, ALU.add)
                stt(y.v(0, 128, 0, 2), cold.v(0, 128, ci * 2, ci * 2 + 2), w(0), y.v(0, 128, 0, 2), ALU.mult, ALU.add)
                stt(y.v(0, 128, 0, 1), cold.v(0, 128, ci * 2 + 1, ci * 2 + 2), w(1), y.v(0, 128, 0, 1), ALU.mult, ALU.add)
                cp("act", cnew.v(0, 128, ci * 2, ci * 2 + 2), pm.v(0, 128, 510, 512))
            act(ys[0].v(), ys[0].v(), AF.Silu)
            tt("pool", ACTT.v(0, 128, c * 512, (c + 1) * 512), ys[0].v(), ys[1].v(), ALU.mult)
        if T + 1 < NT:
            prep_a(out_d, t0 + 512, XIN2, XN2)
        for sbk in range(4):
            pair = (PO[0], PO[1]) if sbk % 2 == 0 else (PSS[0], PSS[1])
            for hf in range(2):
                for c in range(NCH):
                    mm(pair[hf].v(), ACTT.v(0, 128, c * 512 + sbk * 128, c * 512 + (sbk + 1) * 128),
                       WDN.v(0, 128, c * 1024 + hf * 512, c * 1024 + (hf + 1) * 512), c == 0, c == NCH - 1)
            post_norm_residual(pair, out_d, out_d, t0 + sbk * 128, GG2, sbk % 2, XRES2, TT2)
            if sbk == 1 and T + 1 < NT:
                prep_b(16, 24, H2T, XN2, [PT[0], PT[1], RSM0, RSM1])

    P.finalize()
    P.emit()
    return nc


def _rope_tables():
    pos = np.arange(S, dtype=np.float32)
    inv = (np.float32(10000.0) ** (-np.arange(0, 32, 2, dtype=np.float32) / np.float32(32))).astype(np.float32)
    ang = pos[None, :] * inv[:, None]
    cos = np.cos(ang).astype(np.float32)
    sin = np.sin(ang).astype(np.float32)
    cosT = np.concatenate([cos, cos], 0)
    sinT = np.concatenate([-sin, sin], 0)
    return np.tile(cosT, (4, 1)).copy(), np.tile(sinT, (4, 1)).copy()


def _col(v, n):
    return np.ascontiguousarray(np.asarray(v, np.float32).reshape(n, 128).T)


def make_in_maps(inputs):
    f = lambda a: np.ascontiguousarray(np.asarray(a, np.float32))
    cosT, sinT = _rope_tables()
    shared = {
        "w_ada": f(inputs["w_ada"][0]), "b_ada": f(inputs["b_ada"][0]).reshape(1, -1),
        "gpre1": _col(inputs["g_pre_mix"][0], 8), "gpre2": _col(inputs["g_pre_ffn"][0], 8),
        "gpost1": f(inputs["g_post_mix"][0]).reshape(1, -1), "gpost2": f(inputs["g_post_ffn"][0]).reshape(1, -1),
        "w_in": f(inputs["w_in"][0]), "gq": _col(inputs["g_q"][0], 2), "w_uq": f(inputs["w_uq"][0]),
        "gkv": _col(inputs["g_kv"][0], 1), "w_ukv": f(inputs["w_ukv"][0]),
        "lng": f(inputs["gm_ln_g"][0]).reshape(1, 512), "lnb": f(inputs["gm_ln_b"][0]).reshape(1, 512),
        "wsT": np.ascontiguousarray(np.transpose(f(inputs["w_spatial"][0]), (2, 0, 1)).reshape(128, 1024)),
        "bs": np.ascontiguousarray(f(inputs["b_spatial"][0]).T),
        "w_out": f(inputs["w_out"][0]), "w_up": f(inputs["w_up"][0]),
        "convw": np.ascontiguousarray(np.transpose(f(inputs["conv_w"][0]).reshape(3, 44, 128), (2, 1, 0)).reshape(128, 132)),
        "convb": np.ascontiguousarray(f(inputs["conv_b"][0]).reshape(44, 128).T),
        "w_down": f(inputs["w_down"][0]), "cosT": cosT, "sinT": sinT,
    }
    x = f(inputs["x"])
    c = f(inputs["c"])
    maps = []
    for b in range(N_CORES):
        m = dict(shared)
        m["x"] = x[b]
        m["ccol"] = _col(c[b], 8)
        maps.append(m)
    return maps


_NC_CACHE = {}


def kernel(**inputs):
    if "nc" not in _NC_CACHE:
        _NC_CACHE["nc"] = build(8, True)
    nc = _NC_CACHE["nc"]
    in_maps = make_in_maps(inputs)
    res = run_bass_kernel_spmd(nc, in_maps, core_ids=list(range(N_CORES)))
    return np.stack([np.asarray(r["out"], np.float32) for r in res.results], 0)
```
